# Optimizing a Trainium2 kernel written in Bass

```python
import jax, jax.numpy as jnp
from jax import lax
import numpy as np

D_MODEL = 2048
BATCH = 8
SEQ = 2048
DEPTH = 1

ATT_HEADS = 16
ATT_KV_HEADS = 2
HEAD_DIM = 64
ATT_WIDTH = ATT_HEADS * HEAD_DIM
KV_WIDTH = ATT_KV_HEADS * HEAD_DIM
GROUP = ATT_HEADS // ATT_KV_HEADS
WINDOW = 128
BLOCK = 128
ROT_DIM = HEAD_DIM // 4
ROPE_THETA = 500000.0
HG_HEADS = 8
HG_EXPAND = 128
HG_HEAD_V = 128
HG_F_WIDTH = HG_HEADS * HG_EXPAND
HG_V_WIDTH = HG_HEADS * HG_HEAD_V
CHUNK = 64
N_LB = DEPTH + 1
FFN_HIDDEN = ((8 * D_MODEL // 3 + 255) // 256) * 256
N_MOD = 6
IN_COLS = ATT_WIDTH + 2 * KV_WIDTH + 2 * HG_F_WIDTH + 2 * HG_V_WIDTH + 2 * D_MODEL
EPS = 1e-6

kernel_name = "hybrid_swa_sink_hgrn2_gated_block"


def rmsnorm(t, gain):
    t32 = t.astype(jnp.float32)
    y = t32 * lax.rsqrt(jnp.mean(t32 * t32, axis=-1, keepdims=True) + EPS)
    return (y * gain.astype(jnp.float32)).astype(t.dtype)


def rope_partial(t, cos, sin):
    half = ROT_DIM // 2
    t1 = t[..., :half].astype(jnp.float32)
    t2 = t[..., half:ROT_DIM].astype(jnp.float32)
    rot = jnp.concatenate([t1 * cos - t2 * sin, t2 * cos + t1 * sin], axis=-1)
    return jnp.concatenate([rot.astype(t.dtype), t[..., ROT_DIM:]], axis=-1)


def sliding_window_attention(q, k, v, sinks):
    B, S = q.shape[0], q.shape[1]
    nb = S // BLOCK
    qb = q.reshape(B, nb, BLOCK, ATT_KV_HEADS, GROUP, HEAD_DIM).astype(jnp.float32)
    kb = k.reshape(B, nb, BLOCK, ATT_KV_HEADS, HEAD_DIM).astype(jnp.float32)
    vb = v.reshape(B, nb, BLOCK, ATT_KV_HEADS, HEAD_DIM).astype(jnp.float32)
    prev = lambda t: jnp.concatenate([jnp.zeros_like(t[:, :1]), t[:, :-1]], axis=1)
    kk = jnp.concatenate([prev(kb), kb], axis=2)
    vv = jnp.concatenate([prev(vb), vb], axis=2)
    logits = jnp.einsum('bnqhgd,bnkhd->bnhgqk', qb, kk) * (HEAD_DIM ** -0.5)
    qi = jnp.arange(BLOCK)[:, None]
    kj = jnp.arange(2 * BLOCK)[None, :]
    rel = BLOCK + qi - kj
    band = (rel >= 0) & (rel < WINDOW)
    has_prev = (jnp.arange(nb) > 0)[:, None, None] | (kj >= BLOCK)[None]
    mask = band[None] & has_prev
    logits = jnp.where(mask[None, :, None, None], logits, -jnp.inf)
    sink = sinks.astype(jnp.float32).reshape(ATT_KV_HEADS, GROUP)[None, None, :, :, None, None]
    m = jnp.maximum(jnp.max(logits, axis=-1, keepdims=True), sink)
    p = jnp.exp(logits - m)
    denom = jnp.sum(p, axis=-1, keepdims=True) + jnp.exp(sink - m)
    out = jnp.einsum('bnhgqk,bnkhd->bnqhgd', p / denom, vv)
    return out.reshape(B, S, ATT_WIDTH).astype(q.dtype)


def hgrn2_recurrence(q, f_raw, i, lb):
    B, S = q.shape[0], q.shape[1]
    nc = S // CHUNK
    f = lb + (1.0 - lb) * jax.nn.sigmoid(f_raw.astype(jnp.float32))
    key = 1.0 - f
    logf = jnp.log(f)

    def chunks(t, d):
        return t.astype(jnp.float32).reshape(B, nc, CHUNK, HG_HEADS, d).transpose(1, 0, 3, 2, 4)

    qc = chunks(jax.nn.silu(q.astype(jnp.float32)), HG_EXPAND)
    kc = chunks(key, HG_EXPAND)
    vc = chunks(i, HG_HEAD_V)
    bc = jnp.cumsum(chunks(logf, HG_EXPAND), axis=3)
    causal = jnp.tril(jnp.ones((CHUNK, CHUNK), dtype=bool))[:, :, None]

    def step(state, xs):
        qt, kt, vt, bt = xs
        o_inter = jnp.einsum('bhck,bhkv->bhcv', qt * jnp.exp(bt), state)
        diff = bt[:, :, :, None, :] - bt[:, :, None, :, :]
        decay = jnp.where(causal, jnp.exp(jnp.where(causal, diff, 0.0)), 0.0)
        scores = jnp.einsum('bhtk,bhtsk,bhsk->bhts', qt, decay, kt)
        o = o_inter + jnp.einsum('bhts,bhsv->bhtv', scores, vt)
        b_last = bt[:, :, -1:, :]
        new_state = (jnp.exp(b_last[:, :, 0, :])[..., None] * state
                     + jnp.einsum('bhsk,bhsv->bhkv', kt * jnp.exp(b_last - bt), vt))
        return new_state, o

    state0 = jnp.zeros((B, HG_HEADS, HG_EXPAND, HG_HEAD_V), jnp.float32)
    _, o = lax.scan(step, state0, (qc, kc, vc, bc))
    return o.transpose(1, 0, 3, 2, 4).reshape(B, S, HG_HEADS, HG_HEAD_V)


def setup_inputs(seed: int = 0) -> dict:
    key = jax.random.key(seed)
    ks = jax.random.split(key, 20)
    f32 = jnp.float32
    nrm = lambda k, shape, scale: (jax.random.normal(k, shape, f32) * scale)
    gain = lambda k, shape: 1.0 + 0.05 * jax.random.normal(k, shape, f32)
    offsets = jax.random.randint(ks[2], (BATCH, 1), 0, 4096, dtype=jnp.int32)
    positions = (offsets + jnp.arange(SEQ, dtype=jnp.int32)[None, :]).astype(jnp.int32)
    return {
        "x": nrm(ks[0], (BATCH, SEQ, D_MODEL), 1.0),
        "c": nrm(ks[1], (BATCH, D_MODEL), 1.0),
        "positions": positions,
        "w_ada": nrm(ks[3], (DEPTH, D_MODEL, N_MOD * D_MODEL), 0.5 * D_MODEL ** -0.5),
        "b_ada": nrm(ks[4], (DEPTH, N_MOD * D_MODEL), 0.02),
        "g_pre_mix": gain(ks[5], (DEPTH, D_MODEL)),
        "g_post_mix": gain(ks[6], (DEPTH, D_MODEL)),
        "g_pre_ffn": gain(ks[7], (DEPTH, D_MODEL)),
        "g_post_ffn": gain(ks[8], (DEPTH, D_MODEL)),
        "w_in": nrm(ks[9], (DEPTH, D_MODEL, IN_COLS), D_MODEL ** -0.5),
        "attn_sinks": nrm(ks[10], (DEPTH, ATT_HEADS), 1.0),
        "w_attn_proj": nrm(ks[11], (DEPTH, ATT_WIDTH, D_MODEL), ATT_WIDTH ** -0.5),
        "hg_lower_bounds": nrm(ks[12], (N_LB, HG_F_WIDTH), 0.5),
        "hg_norm": gain(ks[13], (DEPTH, HG_HEAD_V)),
        "w_hgrn_proj": nrm(ks[14], (DEPTH, HG_V_WIDTH, D_MODEL), HG_V_WIDTH ** -0.5),
        "w_out": nrm(ks[15], (DEPTH, D_MODEL, D_MODEL), D_MODEL ** -0.5),
        "w_ffn_in": nrm(ks[16], (DEPTH, D_MODEL, 2 * FFN_HIDDEN), D_MODEL ** -0.5),
        "w_ffn_out": nrm(ks[17], (DEPTH, FFN_HIDDEN, D_MODEL), FFN_HIDDEN ** -0.5),
    }


def reference(x, c, positions, w_ada, b_ada, g_pre_mix, g_post_mix, g_pre_ffn, g_post_ffn,
              w_in, attn_sinks, w_attn_proj, hg_lower_bounds, hg_norm, w_hgrn_proj, w_out,
              w_ffn_in, w_ffn_out):
    B, S = x.shape[0], x.shape[1]
    inv_freq = ROPE_THETA ** (-jnp.arange(0, ROT_DIM, 2, dtype=jnp.float32) / ROT_DIM)
    ang = positions.astype(jnp.float32)[..., None] * inv_freq
    cos, sin = jnp.cos(ang)[:, :, None, :], jnp.sin(ang)[:, :, None, :]
    lb_table = jnp.cumsum(jax.nn.softmax(hg_lower_bounds.astype(jnp.float32), axis=0), axis=0)
    splits = [ATT_WIDTH, ATT_WIDTH + KV_WIDTH, ATT_WIDTH + 2 * KV_WIDTH,
              ATT_WIDTH + 2 * KV_WIDTH + HG_F_WIDTH,
              ATT_WIDTH + 2 * KV_WIDTH + 2 * HG_F_WIDTH,
              ATT_WIDTH + 2 * KV_WIDTH + 2 * HG_F_WIDTH + HG_V_WIDTH,
              ATT_WIDTH + 2 * KV_WIDTH + 2 * HG_F_WIDTH + 2 * HG_V_WIDTH,
              ATT_WIDTH + 2 * KV_WIDTH + 2 * HG_F_WIDTH + 2 * HG_V_WIDTH + D_MODEL]
    for l in range(DEPTH):
        mod = (c @ w_ada[l] + b_ada[l])[:, None, :]
        shift1, scale1, gate1, shift2, scale2, gate2 = jnp.split(mod, N_MOD, axis=-1)

        h = rmsnorm(x, g_pre_mix[l]) * (1.0 + scale1) + shift1
        proj = h @ w_in[l]
        q_a, k_a, v_a, q_h, f_h, i_h, g_h, gate_a, gate_h = jnp.split(proj, splits, axis=-1)
        qa = rope_partial(q_a.reshape(B, S, ATT_HEADS, HEAD_DIM), cos, sin)
        ka = rope_partial(k_a.reshape(B, S, ATT_KV_HEADS, HEAD_DIM), cos, sin)
        va = v_a.reshape(B, S, ATT_KV_HEADS, HEAD_DIM)
        y_a = sliding_window_attention(qa, ka, va, attn_sinks[l]) @ w_attn_proj[l]
        o_h = hgrn2_recurrence(q_h, f_h, i_h, lb_table[l])
        o_h = rmsnorm(o_h, hg_norm[l]).reshape(B, S, HG_V_WIDTH).astype(x.dtype)
        y_h = (o_h * jax.nn.sigmoid(g_h)) @ w_hgrn_proj[l]
        merged = jax.nn.sigmoid(gate_a) * y_a + jax.nn.sigmoid(gate_h) * y_h
        y = merged @ w_out[l]
        x = x + gate1 * rmsnorm(y, g_post_mix[l])

        h = rmsnorm(x, g_pre_ffn[l]) * (1.0 + scale2) + shift2
        gu = h @ w_ffn_in[l]
        g_ffn, u_ffn = jnp.split(gu, 2, axis=-1)
        y = (jax.nn.silu(g_ffn) * u_ffn) @ w_ffn_out[l]
        x = x + gate2 * rmsnorm(y, g_post_ffn[l])
    return x
```

```python
import os
import numpy as np
from contextlib import ExitStack
import concourse.bass as bass
import concourse.mybir as mybir
from concourse.bass_utils import run_bass_kernel_spmd

F32 = mybir.dt.float32
BF16 = mybir.dt.bfloat16
F32R = mybir.dt.float32r
USE_F32R = os.environ.get('KF32R', '0') == '1'
KA = os.environ.get('KA', '1') == '1'
KB = os.environ.get('KB', '1') == '1'
KC = os.environ.get('KC', '1') == '1'
I32 = mybir.dt.int32
AF = mybir.ActivationFunctionType
ALU = mybir.AluOpType
AX = mybir.AxisListType

D = 2048
SEQ = 2048
T = 512
NT = 4
FFN = 5632
EPS = 1e-6
NSLOT = 6
HA = 24
NHB = 44
_NT_RUN = NT
_DEBUG = []


class Prog:
    ENG = ["pe", "act", "dve", "pool", "sp"]
    CH = 16000

    def __init__(self):
        self.ops = {e: [] for e in self.ENG}
        self.res = {}
        self.dma_cnt = {}

    def _deps(self, reads, writes):
        deps = set()
        for k in reads:
            st = self.res.get(k)
            if st is not None and st["w"] is not None:
                deps.add(st["w"])
        for k in writes:
            st = self.res.get(k)
            if st is not None:
                if st["w"] is not None:
                    deps.add(st["w"])
                deps.update(st["r"].values())
        return deps

    def _update(self, tok, rid, reads, writes):
        for k in reads:
            st = self.res.setdefault(k, {"w": None, "r": {}})
            st["r"][rid] = tok
        for k in writes:
            self.res[k] = {"w": tok, "r": {}}

    def op(self, eng, fn, reads=(), writes=()):
        idx = len(self.ops[eng])
        deps = self._deps(reads, writes)
        tok = ("e", eng, idx)
        self._update(tok, eng, reads, writes)
        self.ops[eng].append({"fn": fn, "deps": deps, "dma": None})
        return tok

    def dma(self, eng, fn, semkey, reads=(), writes=()):
        deps = self._deps(reads, writes)
        cnt = self.dma_cnt.get(semkey, 0) + 16
        self.dma_cnt[semkey] = cnt
        tok = ("d", semkey, cnt)
        self._update(tok, ("d", semkey), reads, writes)
        self.ops[eng].append({"fn": fn, "deps": deps, "dma": semkey})
        return tok

    def fence(self, old_keys, new_keys):
        acc = {}
        for k in old_keys:
            st = self.res.get(k)
            if st is None:
                continue
            toks = list(st["r"].items())
            if st["w"] is not None:
                w = st["w"]
                toks.append((w[1] if w[0] == "e" else ("d", w[1]), w))
            for rid, tok in toks:
                cur = acc.get(rid)
                if cur is None or tok[2] > cur[2]:
                    acc[rid] = tok
        for k in new_keys:
            st = self.res.get(k)
            merged = dict(acc)
            if st is not None:
                for rid, tok in st["r"].items():
                    cur = merged.get(rid)
                    if cur is None or tok[2] > cur[2]:
                        merged[rid] = tok
                if st["w"] is not None:
                    w = st["w"]
                    rid = w[1] if w[0] == "e" else ("d", w[1])
                    cur = merged.get(rid)
                    if cur is None or w[2] > cur[2]:
                        merged[rid] = w
            self.res[k] = {"w": None, "r": merged}

    def emit(self, nc, es):
        waited = {e: set() for e in self.ENG}
        for e in self.ENG:
            for o in self.ops[e]:
                for d in o["deps"]:
                    if d[0] == "e":
                        if d[1] == "pe" and e == "pe":
                            continue
                        waited[d[1]].add(d[2])
        semval = {}
        nsem = {}
        for e in self.ENG:
            semval[e] = {}
            for c, idx in enumerate(sorted(waited[e])):
                semval[e][idx] = (c // self.CH, c % self.CH + 1)
            nsem[e] = (len(waited[e]) + self.CH - 1) // self.CH
        esems = {e: [es.enter_context(nc.semaphore(f"s_{e}{i}")) for i in range(max(1, nsem[e]))]
                 for e in self.ENG}
        dsems = {k: es.enter_context(nc.semaphore("d_" + "_".join(str(x) for x in (k if isinstance(k, tuple) else (k,)))))
                 for k in self.dma_cnt}
        block = es.enter_context(nc.Block())
        engattr = {"pe": "tensor", "act": "scalar", "dve": "vector", "pool": "gpsimd", "sp": "sync"}

        def make(e):
            def body(eng):
                have = {}
                for idx, o in enumerate(self.ops[e]):
                    need = {}
                    for d in o["deps"]:
                        if d[0] == "e":
                            if d[1] == "pe" and e == "pe":
                                continue
                            ch, v = semval[d[1]][d[2]]
                            for c2 in range(ch):
                                sid = ("e", d[1], c2)
                                need[sid] = self.CH
                            sid = ("e", d[1], ch)
                            need[sid] = max(need.get(sid, 0), v)
                        else:
                            sid = ("d", d[1])
                            need[sid] = max(need.get(sid, 0), d[2])
                    for sid, v in need.items():
                        if v > have.get(sid, 0):
                            sem = esems[sid[1]][sid[2]] if sid[0] == "e" else dsems[sid[1]]
                            eng.wait_ge(sem, v)
                            have[sid] = v
                    if o["fn"] is None:
                        continue
                    ins = o["fn"](eng)
                    if o["dma"] is not None:
                        ins.then_inc(dsems[o["dma"]], 16)
                    elif idx in semval[e]:
                        ch, v = semval[e][idx]
                        ins.then_inc(esems[e][ch], 1)
            return body

        for e in self.ENG:
            if self.ops[e]:
                getattr(block, engattr[e])(make(e))


class Rot:
    def __init__(self, items):
        self.items = list(items)
        self.i = 0

    def next(self):
        v = self.items[self.i % len(self.items)]
        self.i += 1
        return v


W_SPLITS = dict(q_a=(0, 1024), k_a=(1024, 1152), v_a=(1152, 1280), q_h=(1280, 2304), f_h=(2304, 3328),
                i_h=(3328, 4352), g_h=(4352, 5376), gate_a=(5376, 7424), gate_h=(7424, 9472))


def unit_plan():
    plan = []
    for g in ("q0", "q1", "kv"):
        for kq in range(4):
            plan.append(("tm_" + g, kq, 0, 4 * (384 if g == "kv" else 512)))
    for g in ("i0", "i1", "g0", "g1"):
        for kq in range(4):
            plan.append(("tm_" + g, kq, 0, 2048))
    for hd in range(8):
        plan.append(("fm_qh", hd, 0, 2048))
        plan.append(("fm_fh", hd, 0, 2048))
    for cb in range(16):
        plan.append(("fm_ap", cb, 0, 1024))
        plan.append(("fm_ga", cb, 0, 2048))
    for cb in range(16):
        plan.append(("fm_hp", cb, 0, 1024))
        plan.append(("fm_gh", cb, 0, 2048))
    for cbo in range(4):
        for kq in range(4):
            plan.append(("tm_wo", cbo, kq, 2048))
    for half in range(2):
        hbs = range(0, HA) if half == 0 else range(HA, NHB)
        for hb in hbs:
            plan.append(("fm_fg", hb, 0, 2048))
            plan.append(("fm_fu", hb, 0, 2048))
        kgs = range(0, HA // 4) if half == 0 else range(HA // 4, NHB // 4)
        for cbo in range(4):
            for kg in kgs:
                plan.append(("tm_fo", cbo, kg, 2048))
    return plan


def _fm(W, cb, ncol):
    K = W.shape[0]
    nk = K // 128
    blk = W[:, cb * ncol:(cb + 1) * ncol].reshape(nk, 128, ncol).transpose(1, 0, 2)
    return blk.reshape(128, nk * ncol)


def _tm(W, k0, nk):
    ncols = W.shape[1]
    blk = W[k0 * 128:(k0 + nk) * 128, :].reshape(nk, 128, ncols).transpose(1, 0, 2)
    return blk.reshape(128, nk * ncols)


def build_units(w_in, w_attn_proj, w_hgrn_proj, w_out, w_ffn_in, w_ffn_out):
    sp = {k: w_in[:, a:b] for k, (a, b) in W_SPLITS.items()}
    ka = sp["k_a"]
    kv = np.concatenate([ka[:, 0:64], ka[:, 0:64], ka[:, 64:128], ka[:, 64:128], sp["v_a"]], axis=1)
    tmsrc = dict(q0=sp["q_a"][:, 0:512], q1=sp["q_a"][:, 512:1024], kv=kv,
                 i0=sp["i_h"][:, 0:512], i1=sp["i_h"][:, 512:1024],
                 g0=sp["g_h"][:, 0:512], g1=sp["g_h"][:, 512:1024])
    plan = unit_plan()
    units = np.zeros((len(plan), 128, 2048), np.float32)
    for u, (kind, a, b, ne) in enumerate(plan):
        if kind.startswith("tm_") and kind[3:] in tmsrc:
            arr = _tm(tmsrc[kind[3:]], a * 4, 4)
        elif kind == "fm_qh":
            arr = _fm(sp["q_h"], a, 128)
        elif kind == "fm_fh":
            arr = _fm(sp["f_h"], a, 128)
        elif kind == "fm_ap":
            arr = _fm(w_attn_proj, a, 128)
        elif kind == "fm_hp":
            arr = _fm(w_hgrn_proj, a, 128)
        elif kind == "fm_ga":
            arr = _fm(sp["gate_a"], a, 128)
        elif kind == "fm_gh":
            arr = _fm(sp["gate_h"], a, 128)
        elif kind == "tm_wo":
            arr = _tm(w_out[:, a * 512:(a + 1) * 512], b * 4, 4)
        elif kind == "fm_fg":
            arr = _fm(w_ffn_in[:, 0:FFN], a, 128)
        elif kind == "fm_fu":
            arr = _fm(w_ffn_in[:, FFN:2 * FFN], a, 128)
        elif kind == "tm_fo":
            arr = _tm(w_ffn_out[:, a * 512:(a + 1) * 512], b * 4, 4)
        else:
            raise ValueError(kind)
        assert arr.shape[1] == ne, (kind, arr.shape, ne)
        units[u, :, :ne] = arr
    return units


CF = {}
_o = 0
for _n, _w in (("gpre1", 16), ("gpre2", 16), ("hgn", 1), ("sinks", 16), ("lbraw", 16), ("invf", 8),
               ("maskb", 256), ("reset", 512)):
    CF[_n] = (_o, _o + _w)
    _o += _w
NCF = _o


def build_program(nt_run=NT, debug=()):
    nc = bass.Bass("TRN2", target_bir_lowering=False)
    plan = unit_plan()
    NU = len(plan)
    x_d = nc.dram_tensor("x", [SEQ, D], F32, kind="ExternalInput").ap()
    cT_d = nc.dram_tensor("cT", [128, 16], F32, kind="ExternalInput").ap()
    pos_d = nc.dram_tensor("pos", [128, 16], I32, kind="ExternalInput").ap()
    wada_d = nc.dram_tensor("wada", [48, 128, 4096], F32, kind="ExternalInput").ap()
    bada_d = nc.dram_tensor("bada", [24, 1, 512], F32, kind="ExternalInput").ap()
    wts_d = nc.dram_tensor("wts", [NU, 128, 2048], F32, kind="ExternalInput").ap()
    cf_d = nc.dram_tensor("cf", [128, NCF], F32, kind="ExternalInput").ap()
    cb_d = nc.dram_tensor("cbf", [128, 256], F32, kind="ExternalInput").ap()
    gpost_d = nc.dram_tensor("gpost", [2, 128, D], F32, kind="ExternalInput").ap()
    y_d = nc.dram_tensor("y", [SEQ, D], F32, kind="ExternalOutput").ap()
    dbg_d = {}

    P = Prog()
    es = ExitStack()
    with es:
        def sb(name, shape, dt):
            return es.enter_context(nc.sbuf_tensor(name, shape, dt))

        xres = sb("xres", [128, 4, D], F32)
        R3 = sb("R3", [128, 4, D], F32)
        R14 = sb("R14", [128, 16384], BF16)
        hT = sb("hT", [128, 16, T], BF16)
        R5 = sb("R5", [128, 6, T], F32)
        kTd = sb("kTd", [128, 2, SEQ], BF16)
        v_sb = sb("v_sb", [128, 16, 128], BF16)
        state = sb("state", [128, 8, 128], F32)
        G1 = sb("G1", [128, D], F32)
        G2 = sb("G2", [128, D], F32)
        ring = [sb(f"ring{i}", [128, 2048], BF16) for i in range(NSLOT)]
        cf = sb("cfs", [128, NCF], F32)
        cbf = sb("cbfs", [128, 256], BF16)
        tabs = sb("tabs", [128, 4, 16, 8], F32)
        modcol = sb("modcol", [128, 4, 16], F32)
        sm = sb("sm", [128, 512], F32)
        rope_t = sb("rope_t", [128, 4, 64], F32)
        junk = sb("junk", [128, 512], BF16)
        ones = sb("ones", [1, 128], F32)
        cT = sb("cTs", [128, 16], F32)
        posi = sb("posi", [128, 16], I32)
        lbt = sb("lbt", [128, 3, 8], F32)
        ps = [es.enter_context(nc.psum_tensor(f"ps{i}", [128, 512], F32)) for i in range(8)]

        hTf = hT[:, :, :].rearrange("p a b -> p (a b)").bitcast(F32)
        prow = hTf[0:1, 0:1024].rearrange("p (a b) -> p a b", a=2)
        brow = hTf[0:1, 1024:2048].rearrange("p (a b) -> p a b", a=2)
        R3b = R3[:, :, :].rearrange("p a b -> p (a b)").bitcast(BF16)
        R3f = R3[:, :, :].rearrange("p a b -> p (a b)")
        ident = cbf[:, 0:128]
        maskT = cbf[:, 128:256]

        def cfv(name):
            a, b = CF[name]
            return cf[:, a:b]

        dbg_keys = []

        def dump(name, ap, keys):
            if name not in debug or name in dbg_d:
                return
            dt = nc.dram_tensor("dbg_" + name, list(ap.shape), ap.dtype, kind="ExternalOutput").ap()
            dbg_d[name] = dt
            P.dma("sp", lambda e: e.dma_start(out=dt, in_=ap), ("dbg", name), reads=keys, writes=[("dbgout", name)])
            dbg_keys.append(("dbgout", name))

        def mm(out, lhsT, rhs, start, stop, reads, writes):
            P.op("pe", lambda e: e.matmul(out, lhsT, rhs, start=start, stop=stop), reads, writes)

        def tr(out, in_, reads, writes):
            P.op("pe", lambda e: e.transpose(out, in_, ident), list(reads) + ["const2"], writes)

        def act(out, in_, func, reads, writes, bias=None, scale=None, accum=None):
            kw = {}
            if bias is not None:
                kw["bias"] = bias
            if scale is not None:
                kw["scale"] = scale
            if accum is not None:
                kw["accum_out"] = accum
            P.op("act", lambda e: e.activation(out, in_, func, **kw), reads, writes)

        def tt(out, in0, in1, op, reads, writes, eng="dve"):
            P.op(eng, lambda e: e.tensor_tensor(out, in0, in1, op), reads, writes)

        def ts(out, in0, s1, s2, op0, op1, reads, writes, eng="dve"):
            if op1 is None:
                P.op(eng, lambda e: e.tensor_scalar(out, in0, s1, None, op0), reads, writes)
            else:
                P.op(eng, lambda e: e.tensor_scalar(out, in0, s1, s2, op0, op1), reads, writes)

        def stt(out, in0, scalar, in1, op0, op1, reads, writes):
            P.op("dve", lambda e: e.scalar_tensor_tensor(out, in0, scalar, in1, op0, op1), reads, writes)

        def cp(out, in_, reads, writes, eng="dve"):
            P.op(eng, lambda e: e.tensor_copy(out, in_), reads, writes)

        smc = {"i": 0}

        def smcol(n=1):
            i = smc["i"]
            if i + 8 > 480:
                i = 0
            smc["i"] = i + 8
            return sm[:, i:i + n], ("sm", i // 8), i

        ust = {"i": 0}

        def get_unit(kind, a, b=0):
            i = ust["i"]
            ust["i"] += 1
            u = i % NU
            pk, pa, pb, ne = plan[u]
            assert (pk, pa, pb) == (kind, a, b), ((pk, pa, pb), (kind, a, b))
            s = i % NSLOT
            slot = ring[s]
            P.dma("pool", lambda e: e.dma_start(out=slot[:, 0:ne], in_=wts_d[u, :, 0:ne]),
                  ("w", s), reads=(), writes=[("ring", s)])
            return slot, ("ring", s)

        def fmv(slot, nk):
            return slot[:, 0:nk * 128].rearrange("p (k c) -> p k c", k=nk)

        def tmv(slot, ncols):
            return slot[:, 0:4 * ncols].rearrange("p (k c) -> p k c", k=4)

        P.dma("sp", lambda e: e.dma_start(out=cf[:, :], in_=cf_d), "const", writes=["const"])
        P.dma("sp", lambda e: e.dma_start(out=cT[:, :], in_=cT_d), "const", writes=["const"])
        P.dma("sp", lambda e: e.dma_start(out=posi[:, :], in_=pos_d), "const", writes=["const"])
        P.dma("sp", lambda e: e.dma_start(out=R3[:, 0:2, :], in_=gpost_d.rearrange("a p d -> p a d")),
              "const", writes=["const"])
        P.dma("pool", lambda e: e.dma_start(out=cbf[:, :], in_=cb_d), "constb", writes=["constb"])
        P.res["const"]["w"] = ("d", "const", P.dma_cnt["const"])
        P.op("dve", lambda e: e.memset(ones[:, :], 1.0), writes=["ones"])
        P.op("dve", lambda e: e.memset(state[:, :, :], 0.0), writes=[("state", h) for h in range(8)])
        P.op("dve", lambda e: e.memset(sm[:, 500:501], EPS), writes=["eps"])
        epsc = sm[:, 500:501]
        P.op("dve", lambda e: e.memset(sm[:, 501:502], 0.0), reads=["constb"], writes=["const2"])

        STAGE = int(os.environ.get('KSTAGE', '99'))
        lbraw = cfv("lbraw")
        tt(lbt[:, 0, :], lbraw[:, 0:8], lbraw[:, 8:16], ALU.subtract, ["const"], ["lbt"])
        act(lbt[:, 0, :], lbt[:, 0, :], AF.Sigmoid, ["lbt"], ["lbt"])
        ts(lbt[:, 1, :], lbt[:, 0, :], -1.0, 1.0, ALU.mult, ALU.add, ["lbt"], ["lbt1"])
        ts(lbt[:, 2, :], lbt[:, 0, :], 1.0, -1.0, ALU.mult, ALU.add, ["lbt", "lbt1"], ["lbt2"])

        ang = R5[:, 0, 0:128]
        kf = R5[:, 0, 128:256]
        ki = R5[:, 0, 256:384].bitcast(I32)
        r1 = R5[:, 0, 384:512]
        r2 = R5[:, 1, 0:128]
        mk = R5[:, 1, 128:256]
        posf = R5[:, 1, 256:272]
        if STAGE < 2:
            P.emit(nc, es)
            return nc
        cp(posf, posi[:, :], ["const"], ["rp0"])
        TWO_PI = float(2.0 * np.pi)
        C1 = 6.28125
        C2 = float(2.0 * np.pi - 6.28125)
        PI = float(np.float32(np.pi))
        tt(ang.rearrange("p (n j) -> p n j", j=8), posf.unsqueeze(2).to_broadcast([128, 16, 8]),
           cfv("invf").unsqueeze(1).to_broadcast([128, 16, 8]), ALU.mult, ["rp0", "const"], ["rp1"])
        ts(ki, ang, 1.0 / TWO_PI, None, ALU.mult, None, ["rp1"], ["rp2"])
        cp(kf, ki, ["rp2"], ["rp3"])
        stt(r1, kf, -C1, ang, ALU.mult, ALU.add, ["rp3", "rp1"], ["rp4"])
        stt(r2, kf, -C2, r1, ALU.mult, ALU.add, ["rp3", "rp4"], ["rp5"])

        def wrap(dst, src, rk, wk):
            P.op("dve", lambda e: e.tensor_single_scalar(mk, src, PI, ALU.is_gt), rk, ["rpm"])
            stt(dst, mk, -TWO_PI, src, ALU.mult, ALU.add, list(rk) + ["rpm"], ["rpw"])
            P.op("dve", lambda e: e.tensor_single_scalar(mk, dst, -PI, ALU.is_lt), ["rpw"], ["rpm"])
            stt(dst, mk, TWO_PI, dst, ALU.mult, ALU.add, ["rpw", "rpm"], wk)

        rs = R5[:, 2, 0:128]
        rc = R5[:, 2, 128:256]
        wrap(rs, r2, ["rp5"], ["rp6"])
        ts(rc, rs, float(np.pi / 2), None, ALU.add, None, ["rp6"], ["rp7"])
        wrap(rc, rc, ["rp7"], ["rp8"])
        sinv = tabs[:, 3, :, :].rearrange("p n j -> p (n j)")
        cosv = tabs[:, 2, :, :].rearrange("p n j -> p (n j)")
        act(sinv, rs, AF.Sin, ["rp6"], ["tab_s"])
        act(cosv, rc, AF.Sin, ["rp8"], ["tab_c"])
        ts(tabs[:, 0, :, :].rearrange("p n j -> p (n j)"), cosv, 0.125, None, ALU.mult, None, ["tab_c"], ["tab_cq"])
        ts(tabs[:, 1, :, :].rearrange("p n j -> p (n j)"), sinv, 0.125, None, ALU.mult, None, ["tab_s"], ["tab_sq"])
        TABK = ["tab_s", "tab_c", "tab_cq", "tab_sq"]

        xres_f = xres[:, :, :].rearrange("p a b -> p (a b)")
        R14f = R14[:, :].bitcast(F32)
        NSTG = 8
        stg = [R14[:, i * 2048:(i + 1) * 2048] for i in range(NSTG)]
        stgk = [("stg", i) for i in range(NSTG)]
        cTb = sb("cTb", [128, 16], BF16)
        cp(cTb[:, :], cT[:, :], ["const"], ["cTb"])
        gp = R3
        for cbk in range(24 if STAGE >= 3 else 0):
            bank = ps[cbk % 2]
            rb = cbk % 2
            P.dma("sp", lambda e, cbk=cbk, rb=rb: e.dma_start(out=brow[:, rb, :], in_=bada_d[cbk]),
                  ("brow", rb), writes=[("brow", rb)])
            for kh in range(4):
                pi = cbk * 4 + kh
                si = pi % NSTG
                P.dma("pool", lambda e, pi=pi, si=si: e.dma_start(
                    out=stg[si], in_=wada_d[pi // 2, :, (pi % 2) * 2048:(pi % 2 + 1) * 2048]),
                      ("stg", si), writes=[stgk[si]])
                sv = stg[si].rearrange("p (k c) -> p k c", k=4)
                for k in range(4):
                    kk = kh * 4 + k
                    mm(bank[0:1, :], cTb[:, kk:kk + 1], sv[:, k, :], kk == 0, False,
                       [stgk[si], "cTb"], [("ps", cbk % 2)])
            mm(bank[0:1, :], ones[0:1, 0:1], brow[:, rb, :], False, True, [("brow", rb), "ones"], [("ps", cbk % 2)])
            cp(prow[:, rb, :], bank[0:1, :], [("ps", cbk % 2)], [("prow", rb)])
            which = cbk // 4
            q4 = cbk % 4
            bb = ps[2 + cbk % 2]
            bk = ("ps", 2 + cbk % 2)
            mm(bb[:, :], ones[0:1, :], prow[:, rb, :], True, True, [("prow", rb), "ones"], [bk])
            if which in (2, 5):
                G = G1 if which == 2 else G2
                gi = 0 if which == 2 else 1
                tt(G[:, q4 * 512:(q4 + 1) * 512], bb[:, :], gp[:, gi, q4 * 512:(q4 + 1) * 512], ALU.mult,
                   [bk, "const"], [("G", gi)])
            else:
                dgt = R5[:, 3, :].rearrange("p (c f) -> p c f", c=4)
                tt(dgt, bb[:, :].rearrange("p (c f) -> p c f", c=4), ident.unsqueeze(1).to_broadcast([128, 4, 128]),
                   ALU.mult, [bk, "const2"], ["dgt"])
                dcol, dck, _ = smcol(4)
                P.op("dve", lambda e, o=dcol, i=dgt: e.reduce_sum(o, i, AX.X), ["dgt"], [dck])
                cs = slice(q4 * 4, q4 * 4 + 4)
                if which == 0:
                    cp(modcol[:, 1, cs], dcol, [dck], ["modcol"])
                elif which == 3:
                    cp(modcol[:, 3, cs], dcol, [dck], ["modcol"])
                else:
                    gi = 0 if which == 1 else 2
                    gpre = cfv("gpre1") if which == 1 else cfv("gpre2")
                    stt(modcol[:, gi, cs], dcol, 1.0, gpre[:, cs], ALU.add, ALU.mult,
                        [dck, "const"], ["modcol"])

        XK = [("x", j) for j in range(4)]
        XNK = [("xn", j) for j in range(4)]
        ATK = [("attT", j) for j in range(4)] + [("ogT", h) for h in range(8)]
        P.fence(stgk, XK + XNK + ATK + ["mergedT", "actT"])
        P.fence([("prow", 0), ("prow", 1), ("brow", 0), ("brow", 1)], ["hT"])
        YK = [("ybuf", j) for j in range(4)]
        P.fence(["const"], YK)
        AK = []
        HK = []

        xn = R14[:, 0:8192].rearrange("p (j d) -> p j d", j=4)
        mergedT = R14[:, 0:8192].rearrange("p (c t) -> p c t", c=16)
        attT = R14[:, 8192:12288].rearrange("p (c t) -> p c t", c=8)
        ogT = R14[:, 12288:16384].rearrange("p (c t) -> p c t", c=8)
        actT = R14[:, 0:12288].rearrange("p (c t) -> p c t", c=24)
        ybuf = R3
        qtm = R3b[:, 0:4096].rearrange("p (j d) -> p j d", j=4)
        qT = R3b[:, 4096:8192].rearrange("p (c t) -> p c t", c=8)
        mlogs = [R3f[:, 4096:5120].rearrange("p (h k) -> p h k", h=4), R3f[:, 5120:6144].rearrange("p (h k) -> p h k", h=4),
                 R3f[:, 0:1024].rearrange("p (h k) -> p h k", h=4), R3f[:, 1024:2048].rearrange("p (h k) -> p h k", h=4)]
        pn = R3b[:, 12288:14336].rearrange("p (b h k) -> p b h k", b=2, h=4)
        pT = R3b[:, 14336:16384].rearrange("p (b h k) -> p b h k", b=2, h=4)
        ktm = sb("ktm", [128, 256], BF16)
        vh = R14[:, 0:4096].rearrange("p (j d) -> p j d", j=4)
        ghs = R14[:, 4096:8192].rearrange("p (j d) -> p j d", j=4)
        S1 = R3b[:, 0:8192].rearrange("p (c t) -> p c t", c=16)
        qtl = R3b[:, 8192:12288].rearrange("p (h t) -> p h t", h=8)
        ktl = R3b[:, 12288:16384].rearrange("p (h t) -> p h t", h=8)
        hsm = sb("hsm", [128, 8, 6, 8], F32)
        stb8 = sb("stb8", [128, 8, 128], BF16)
        scs8 = sb("scs8", [128, 8, 64], BF16)
        ktk8 = sb("ktk8", [128, 8, 128], BF16)
        ogtm = sb("ogtm", [128, 8, 128], BF16)
        ssm = sb("ssm", [128, 64], F32)

        AKEYS = [("qtm", j) for j in range(4)] + [("qT", j) for j in range(4)] + \
                [("mlog", b) for b in range(4)] + [("pn", b, hh) for b in range(2) for hh in range(4)] + [("pT", b) for b in range(2)]
        VGK = [("vh", j) for j in range(4)] + [("ghs", j) for j in range(4)]
        HKEYS = [("qtl", h) for h in range(8)] + [("ktl", h) for h in range(8)] + ["S1"]
        R5K = [("R5", i) for i in range(6)]
        P.fence(["rp0", "rp1", "rp2", "rp3", "rp4", "rp5", "rp6", "rp7", "rp8", "rpm", "rpw", "dgt"], R5K)

        bankset = Rot([(0, 1, 2, 3), (4, 5, 6, 7)])
        trb = Rot([4, 5])
        trb4 = Rot([4, 5, 6, 7])
        def rstd_from_ss(ss_ap, ss_keys, scale):
            t1, k1, _ = smcol()
            t2, k2, _ = smcol()
            act(t1, ss_ap, AF.Ln, list(ss_keys) + ["eps"], [k1], bias=epsc, scale=scale)
            act(t2, t1, AF.Exp, [k1], [k2], scale=-0.5)
            return t2, k2

        def load_x(tile, j):
            r0 = tile * T + j * 128
            P.dma("sp", lambda e, j=j, r0=r0: e.dma_start(out=xres[:, j, :], in_=x_d[r0:r0 + 128, :]),
                  ("xin", j), writes=[("x", j)])

        def norm_to_hT(gi, si, load_tile=None):
            for j in range(4):
                ssc, ssk, _ = smcol()
                act(xn[:, j, :], xres[:, j, :], AF.Square, [("x", j)], [("xn", j), ssk], accum=ssc)
                rs_, rk_ = rstd_from_ss(ssc, [ssk], 1.0 / D)
                ts(xn[:, j, :], xres[:, j, :], rs_, None, ALU.mult, None, [("x", j), rk_, ("xn", j)], [("xn", j)])
            P.fence(["hT"], [("hTw", 0), ("hTw", 1)])
            for c in range(16):
                b = trb4.next()
                pb = ps[b][:, :].bitcast(BF16)
                for j in range(4):
                    tr(pb[:, j * 128:(j + 1) * 128], xn[:, j, c * 128:(c + 1) * 128], [("xn", j)], [("ps", b)])
                if c % 2 == 0:
                    act(hT[:, c, :], pb[:, 0:512], AF.Identity, [("ps", b), "modcol"], [("hTw", 0)],
                        bias=modcol[:, si, c:c + 1], scale=modcol[:, gi, c:c + 1])
                else:
                    ts(hT[:, c, :], pb[:, 0:512], modcol[:, gi, c:c + 1], modcol[:, si, c:c + 1], ALU.mult, ALU.add,
                       [("ps", b), "modcol"], [("hTw", 1)])
            P.op("dve", lambda e: e.memset(sm[:, 502:503], 0.0), [("hTw", 0), ("hTw", 1)], ["hT"])

        def tm_group(kind, a, ncols, lhs_of, lhs_keys, nk_units, evac):
            banks = bankset.next()
            for kq in range(nk_units):
                if kind in ("tm_wo", "tm_fo"):
                    slot, sk = get_unit(kind, a[0], a[1] + kq)
                else:
                    slot, sk = get_unit(kind, kq)
                wv = tmv(slot, ncols)
                for j in range(4):
                    for k in range(4):
                        mm(ps[banks[j]][:, 0:ncols], lhs_of(kq * 4 + k, j), wv[:, k, :],
                           kq == 0 and k == 0, kq == nk_units - 1 and k == 3,
                           [sk] + list(lhs_keys), [("ps", banks[j])])
            for j in range(4):
                evac(j, ps[banks[j]], ("ps", banks[j]))

        def rope(psv, nh, dst, tq, n, rkeys, wkeys):
            C = tabs[:, tq, n:n + 1, :].to_broadcast([128, nh, 8])
            S_ = tabs[:, tq + 1, n:n + 1, :].to_broadcast([128, nh, 8])
            a = psv[:, :, 0:8]
            b = psv[:, :, 8:16]
            tmp = [rope_t[:, i, 0:nh * 8].rearrange("p (h e) -> p h e", e=8) for i in range(4)]
            rk = list(rkeys) + TABK
            tt(tmp[0], a, C, ALU.mult, rk, [("rt", 0)])
            tt(tmp[1], b, S_, ALU.mult, rk, [("rt", 1)])
            tt(dst[:, :, 0:8], tmp[0], tmp[1], ALU.subtract, [("rt", 0), ("rt", 1)], wkeys)
            tt(tmp[2], b, C, ALU.mult, rk, [("rt", 2)])
            tt(tmp[3], a, S_, ALU.mult, rk, [("rt", 3)])
            tt(dst[:, :, 8:16], tmp[2], tmp[3], ALU.add, [("rt", 2), ("rt", 3)], wkeys)
            act(dst[:, :, 16:64], psv[:, :, 16:64], AF.Copy, rkeys, wkeys, scale=(0.125 if tq == 0 else 1.0))

        for j in range(4):
            if nt_run > 0:
                load_x(0, j)
        for t in range(nt_run):
            P.fence(["actT"], XNK)
            norm_to_hT(0, 1, load_tile=t)

            if STAGE == 10:
                break
            dump('hT1', hT[:, :, :], ['hT'])
            dump('modcol', modcol[:, :, :], ['modcol'])
            dump('G1', G1[:, :], [('G', 0)])
            dump('tabs', tabs[:, :, :, :], TABK)
            P.fence(YK, AKEYS)
            P.fence(["actT"], ATK)

            def lhs_h(kk, j):
                return hT[:, kk, j * 128:(j + 1) * 128]

            for g in range(2):
                def ev_q(j, bank, bk, g=g):
                    n = 4 * t + j
                    rope(bank[:, :].rearrange("p (h d) -> p h d", d=64), 8,
                         qtm[:, j, g * 512:(g + 1) * 512].rearrange("p (h d) -> p h d", d=64), 0, n,
                         [bk], [("qtm", j)])
                tm_group("tm_q%d" % g, None, 512, lhs_h, ["hT"], 4, ev_q)

            def ev_kv(j, bank, bk):
                n = 4 * t + j
                rope(bank[:, 0:256].rearrange("p (h d) -> p h d", d=64), 4,
                     ktm[:, :].rearrange("p (h d) -> p h d", d=64), 2, n, [bk], ["ktm"])
                act(v_sb[:, n, :], bank[:, 256:384], AF.Copy, [bk], [("v", n)])
                b = trb.next()
                pb = ps[b][:, :].bitcast(BF16)
                for kvh in range(2):
                    tr(pb[:, kvh * 128:(kvh + 1) * 128], ktm[:, kvh * 128:(kvh + 1) * 128], ["ktm"], [("ps", b)])
                cp(kTd[:, :, n * 128:(n + 1) * 128], pb[:, 0:256].rearrange("p (a b) -> p a b", a=2),
                   [("ps", b)], [("kT", n)])
            tm_group("tm_kv", None, 384, lhs_h, ["hT"], 4, ev_kv)

            for j in range(4):
                b = trb.next()
                pb = ps[b][:, :].bitcast(BF16)
                for pr in range(8):
                    tr(pb[:, pr * 128:(pr + 1) * 128], qtm[:, j, pr * 128:(pr + 1) * 128], [("qtm", j)], [("ps", b)])
                act(qT[:, :, j * 128:(j + 1) * 128], pb[:, :].rearrange("p (c t) -> p c t", c=8), AF.Copy,
                    [("ps", b)], [("qT", j)])

            if STAGE == 11:
                break
            dump('qT', qT[:, :, :], [('qT', j) for j in range(4)])
            dump('kTd', kTd[:, :, 0:512], [('kT', n) for n in range(4)])
            dump('v_sb', v_sb[:, 0:4, :], [('v', n) for n in range(4)])
            P.fence([("qtm", j) for j in range(4)], [("mlog", 2), ("mlog", 3)])
            lgb = Rot([6, 7])
            sinkbc = cfv("sinks")
            maskb = cfv("maskb")
            groups = [(j, hg) for j in range(4) for hg in range(4)]
            gctx = {}

            def att_L(s_):
                j, hg = groups[s_]
                n = 4 * t + j
                nk = 256 if n > 0 else 128
                k0 = (n - 1) * 128 if n > 0 else 0
                buf = s_ % 4
                ml = mlogs[buf]
                c = {"nk": nk, "n": n, "j": j, "hg": hg, "buf": buf}
                c["mx"], c["mxk"], _ = smcol(4)
                c["ng"], c["ngk"], _ = smcol(4)
                c["rs"], c["rsk"], _ = smcol(4)
                c["es"], c["esk"], _ = smcol(4)
                c["rd"], c["rdk"], _ = smcol(4)
                gctx[s_] = c
                lbs = [lgb.next(), lgb.next()]
                for hh in range(4):
                    h = hg * 4 + hh
                    pr, half, kvh = h // 2, h % 2, h // 8
                    lb_ = lbs[hh % 2]
                    psl = ps[lb_][:, (hh // 2) * 256:(hh // 2) * 256 + nk]
                    mm(psl, qT[half * 64:(half + 1) * 64, pr, j * 128:(j + 1) * 128],
                       kTd[half * 64:(half + 1) * 64, kvh, k0:k0 + nk], True, True,
                       [("qT", j)] + [("kT", nn) for nn in ((n - 1, n) if n > 0 else (n,))], [("ps", lb_)])
                if KA:
                    ml2 = ml.rearrange("p (a b) k -> p a b k", b=2)
                    for b2 in range(2):
                        lb_ = lbs[b2]
                        tt(ml2[:, :, b2, 0:nk],
                           ps[lb_][:, :].rearrange("p (h k) -> p h k", h=2)[:, :, 0:nk],
                           maskb[:, 256 - nk:256].unsqueeze(1).to_broadcast([128, 2, nk]), ALU.add,
                           [("ps", lb_), "const"], [("mlog", buf)])
                else:
                    for hh in range(4):
                        lb_ = lbs[hh % 2]
                        tt(ml[:, hh, 0:nk], ps[lb_][:, (hh // 2) * 256:(hh // 2) * 256 + nk], maskb[:, 256 - nk:256],
                           ALU.add, [("ps", lb_), "const"], [("mlog", buf)])
                if KB:
                    P.op("dve", lambda e, o=c["mx"], i=ml[:, :, 0:nk]: e.reduce_max(o, i, AX.X),
                         [("mlog", buf)], [c["mxk"]])
                else:
                    for hh in range(4):
                        P.op("dve", lambda e, o=c["mx"][:, hh:hh + 1], i=ml[:, hh, 0:nk]: e.reduce_max(o, i, AX.X),
                             [("mlog", buf)], [c["mxk"]])
                tt(c["mx"], c["mx"], sinkbc[:, hg * 4:hg * 4 + 4], ALU.max, [c["mxk"], "const"], [c["mxk"]])
                ts(c["ng"], c["mx"], -1.0, None, ALU.mult, None, [c["mxk"]], [c["ngk"]])
                tt(c["es"], sinkbc[:, hg * 4:hg * 4 + 4], c["mx"], ALU.subtract, [c["mxk"], "const"], [c["esk"]])
                for hh in range(4):
                    act(ml[:, hh, 0:nk], ml[:, hh, 0:nk], AF.Exp, [("mlog", buf), c["ngk"]],
                        [("mlog", buf), c["rsk"]], bias=c["ng"][:, hh:hh + 1], accum=c["rs"][:, hh:hh + 1])
                act(c["es"], c["es"], AF.Exp, [c["esk"]], [c["esk"]])

            def att_P2(s_):
                c = gctx[s_]
                nk, buf = c["nk"], c["buf"]
                ml = mlogs[buf]
                tt(c["rd"], c["rs"], c["es"], ALU.add, [c["rsk"], c["esk"]], [c["rdk"]])
                P.op("dve", lambda e, o=c["rd"], i=c["rd"]: e.reciprocal(o, i), [c["rdk"]], [c["rdk"]])
                for hh in range(4):
                    if hh % 2 == 0:
                        act(pn[:, s_ % 2, hh, 0:nk], ml[:, hh, 0:nk], AF.Copy, [("mlog", buf), c["rdk"]],
                            [("pn", s_ % 2, hh)], scale=c["rd"][:, hh:hh + 1])
                    else:
                        ts(pn[:, s_ % 2, hh, 0:nk], ml[:, hh, 0:nk], c["rd"][:, hh:hh + 1], None, ALU.mult, None,
                           [("mlog", buf), c["rdk"]], [("pn", s_ % 2, hh)])

            def att_T(s_):
                c = gctx[s_]
                nk = c["nk"]
                b2_ = s_ % 2
                b = trb.next()
                pb = ps[b][:, :].bitcast(BF16)
                for hh in range(4):
                    for kk in range(nk // 128):
                        tr(pb[:, hh * 256 + kk * 128: hh * 256 + (kk + 1) * 128],
                           pn[:, b2_, hh, kk * 128:(kk + 1) * 128], [("pn", b2_, hh)], [("ps", b)])
                if nk == 256:
                    if KC:
                        cp(pT[:, b2_, :, :], pb[:, :].rearrange("p (h k) -> p h k", h=4), [("ps", b)], [("pT", b2_)])
                    else:
                        act(pT[:, b2_, :, :], pb[:, :].rearrange("p (h k) -> p h k", h=4), AF.Copy,
                            [("ps", b)], [("pT", b2_)])
                else:
                    act(pT[:, b2_, :, 0:128], pb[:, :].rearrange("p (h k) -> p h k", h=4)[:, :, 0:128], AF.Copy,
                        [("ps", b)], [("pT", b2_)])

            def att_PV(s_):
                c = gctx[s_]
                n, j, hg = c["n"], c["j"], c["hg"]
                b2_ = s_ % 2
                pvb = (0, 1)
                for hh in range(4):
                    h = hg * 4 + hh
                    pr, half, kvh = h // 2, h % 2, h // 8
                    ob = pvb[pr // 4]
                    out = ps[ob][half * 64:(half + 1) * 64, (pr % 4) * 128:(pr % 4 + 1) * 128]
                    if n > 0:
                        mm(out, v_sb[:, n - 1, kvh * 64:(kvh + 1) * 64], pT[:, b2_, hh, 0:128], True, False,
                           [("v", n - 1), ("pT", b2_)], [("ps", ob)])
                        mm(out, v_sb[:, n, kvh * 64:(kvh + 1) * 64], pT[:, b2_, hh, 128:256], False, True,
                           [("v", n), ("pT", b2_)], [("ps", ob)])
                    else:
                        mm(out, v_sb[:, n, kvh * 64:(kvh + 1) * 64], pT[:, b2_, hh, 0:128], True, True,
                           [("v", n), ("pT", b2_)], [("ps", ob)])
                if hg == 3:
                    for q2 in range(2):
                        act(attT[:, q2 * 4:(q2 + 1) * 4, j * 128:(j + 1) * 128],
                            ps[pvb[q2]][:, :].rearrange("p (c t) -> p c t", c=4), AF.Copy,
                            [("ps", pvb[q2])], [("attT", j)])

            P.fence(XNK, VGK)
            pieces = []

            def mk_piece(kind, units, jp, kq, dst, dkey, func):
                def piece():
                    if jp == 0:
                        units.append(get_unit(kind, kq))
                    slot, sk = units[kq]
                    wv = tmv(slot, 512)
                    for j in (2 * jp, 2 * jp + 1):
                        b = 2 + j % 2
                        for k in range(4):
                            mm(ps[b][:, :], hT[:, kq * 4 + k, j * 128:(j + 1) * 128], wv[:, k, :],
                               kq == 0 and k == 0, kq == 3 and k == 3, [sk, "hT"], [("ps", b)])
                    if kq == 3:
                        for j in (2 * jp, 2 * jp + 1):
                            b = 2 + j % 2
                            if func is None:
                                cp(dst(j), ps[b][:, :], [("ps", b)], [(dkey, j)])
                            else:
                                act(dst(j), ps[b][:, :], func, [("ps", b)], [(dkey, j)])
                return piece

            for g in range(2):
                units = []
                for jp in range(2):
                    for kq in range(4):
                        pieces.append(mk_piece("tm_i%d" % g, units, jp, kq,
                                               (lambda j, g=g: vh[:, j, g * 512:(g + 1) * 512]), "vh", None))
            for g in range(2):
                units = []
                for jp in range(2):
                    for kq in range(4):
                        pieces.append(mk_piece("tm_g%d" % g, units, jp, kq,
                                               (lambda j, g=g: ghs[:, j, g * 512:(g + 1) * 512]), "ghs", AF.Sigmoid))

            att_L(0)
            for step in range(16 + 3):
                if 0 <= step - 1 < 16:
                    att_P2(step - 1)
                if 0 <= step - 2 < 16:
                    att_T(step - 2)
                if 0 <= step - 3 < 16:
                    att_PV(step - 3)
                if step + 1 < 16:
                    att_L(step + 1)
                for _ in range(2):
                    if pieces:
                        pieces.pop(0)()
            while pieces:
                pieces.pop(0)()

            dump('attT', attT[:, :, :], [('attT', j) for j in range(4)])
            P.fence(AKEYS, HKEYS)

            if STAGE == 13:
                break
            fmb = Rot([(0, 1), (2, 3)])
            reset = cfv("reset")
            for hd in range(8):
                bq, bf = fmb.next()
                sq_, kq_ = get_unit("fm_qh", hd)
                wq = fmv(sq_, 16)
                for k in range(16):
                    mm(ps[bq][:, :], wq[:, k, :], hT[:, k, :], k == 0, k == 15, [kq_, "hT"], [("ps", bq)])
                sf_, kf_ = get_unit("fm_fh", hd)
                wf = fmv(sf_, 16)
                for k in range(16):
                    mm(ps[bf][:, :], wf[:, k, :], hT[:, k, :], k == 0, k == 15, [kf_, "hT"], [("ps", bf)])
                A = [R5[:, i, :] for i in range(6)]
                K5 = R5K
                act(A[0], ps[bf][:, :], AF.Sigmoid, [("ps", bf)], [K5[0]])
                act(A[4], ps[bq][:, :], AF.Sigmoid, [("ps", bq)], [K5[4]])
                tt(A[4], A[4], ps[bq][:, :], ALU.mult, [K5[4], ("ps", bq)], [K5[4]])
                act(A[1], A[0], AF.Ln, [K5[0], "lbt", "lbt1"], [K5[1]],
                    bias=lbt[:, 0, hd:hd + 1], scale=lbt[:, 1, hd:hd + 1])
                ts(A[2], A[0], -1.0, lbt[:, 2, hd:hd + 1], ALU.add, ALU.mult, [K5[0], "lbt2"], [K5[2]])
                P.op("dve", lambda e, o=A[3], d0=reset, d1=A[1]: e.tensor_tensor_scan(o, d0, d1, 0.0, ALU.mult, ALU.add),
                     [K5[1], "const"], [K5[3]])
                bv = A[3].rearrange("p (c t) -> p c t", t=64)
                hs = hsm[:, hd, :, :]
                hk = ("hsm", hd)
                tt(hs[:, 5, :], bv[:, :, 63], bv[:, :, 31], ALU.subtract, [K5[3]], [(hk, 5)])
                act(hs[:, 2, :], bv[:, :, 31], AF.Exp, [K5[3]], [(hk, 2)])
                act(hs[:, 3, :], hs[:, 5, :], AF.Exp, [(hk, 5)], [(hk, 3)])
                act(hs[:, 4, :], bv[:, :, 63], AF.Exp, [K5[3]], [(hk, 4)])
                tt(A[1].rearrange("p (c t) -> p c t", t=64), bv, bv[:, :, 31:32].to_broadcast([128, 8, 64]),
                   ALU.subtract, [K5[3], K5[1]], [K5[1]])
                act(A[5], A[1], AF.Exp, [K5[1]], [K5[5]])
                act(A[1], A[1], AF.Exp, [K5[1]], [K5[1]], scale=-1.0)
                tt(qtl[:, hd, :], A[4], A[5], ALU.mult, [K5[4], K5[5]], [("qtl", hd)])
                tt(ktl[:, hd, :], A[2], A[1], ALU.mult, [K5[2], K5[1]], [("ktl", hd)])

            if STAGE == 14:
                break
            dump('qtl', qtl[:, :, :], [('qtl', h) for h in range(8)])
            dump('ktl', ktl[:, :, :], [('ktl', h) for h in range(8)])
            dump('vh', vh[:, :, :], [('vh', j) for j in range(4)])
            dump('ghs', ghs[:, :, :], [('ghs', j) for j in range(4)])
            dump('hsm', hsm[:, :, :, :], [(('hsm', h), i) for h in range(8) for i in (2, 3, 4, 5)])
            hrot = Rot([0, 1])
            hgn = cfv("hgn")
            BIG = 1.0e30
            ATTK = [("attT", j) for j in range(4)]
            OGK = [("ogT", h) for h in range(8)]

            def early_M_pe(cb):
                s1, k1 = get_unit("fm_ap", cb)
                w1 = fmv(s1, 8)
                for k in range(8):
                    mm(ps[2][:, :], w1[:, k, :], attT[:, k, :], k == 0, k == 7, [k1] + ATTK, [("ps", 2)])
                s2, k2 = get_unit("fm_ga", cb)
                w2 = fmv(s2, 16)
                for k in range(16):
                    mm(ps[3][:, :], w2[:, k, :], hT[:, k, :], k == 0, k == 15, [k2, "hT"], [("ps", 3)])

            def early_M_evac(cb):
                act(R5[:, 5, :], ps[3][:, :], AF.Sigmoid, [("ps", 3)], [R5K[5]])
                tt(S1[:, cb, :], R5[:, 5, :], ps[2][:, :], ALU.mult, [R5K[5], ("ps", 2)], ["S1"])

            STK = [("state", h) for h in range(8)]
            HSK = lambda i: [(("hsm", h), i) for h in range(8)]
            QK = [("qtl", h) for h in range(8)]
            KK = [("ktl", h) for h in range(8)]
            otmp8 = R5[:, 0:2, :].rearrange("p a (h v) -> p (a h) v", v=128)
            tmp8 = R5[:, 2:4, :].rearrange("p a (h v) -> p (a h) v", v=128)
            for c64 in range(8):
                jb, hf = c64 // 2, c64 % 2
                sl = slice(c64 * 64, (c64 + 1) * 64)
                pp = slice(hf * 64, (hf + 1) * 64)
                vhv = vh[:, jb, :].rearrange("p (h v) -> p h v", v=128)
                ghv = ghs[:, jb, :].rearrange("p (h v) -> p h v", v=128)
                bA = hrot.next()
                for hd in range(8):
                    mm(ps[bA][pp, hd * 64:(hd + 1) * 64], ktl[:, hd, sl], qtl[:, hd, sl], True, True,
                       [("ktl", hd), ("qtl", hd)], [("ps", bA)])
                ctmp = R5[pp, 4, :]
                ts(ctmp, ps[bA][pp, :], BIG, -BIG, ALU.min, ALU.max, [("ps", bA)], [R5K[4]])
                tt(scs8[pp, :, :], ctmp.rearrange("p (h t) -> p h t", t=64),
                   maskT[pp, hf * 64:(hf + 1) * 64].unsqueeze(1).to_broadcast([64, 8, 64]), ALU.mult,
                   [R5K[4], "const2"], ["scs8"])
                bB = hrot.next()
                pbB = ps[bB][:, :].bitcast(BF16)
                for hd in range(8):
                    tr(pbB[pp, hd * 128:(hd + 1) * 128], ktl[:, hd, sl], [("ktl", hd)], [("ps", bB)])
                act(ktk8[pp, :, :], pbB[pp, :].rearrange("p (h k) -> p h k", k=128), AF.Copy, [("ps", bB)], ["ktk8"])
                for hd in range(8):
                    ub = 6 + hd // 4
                    mm(ps[ub][:, (hd % 4) * 128:(hd % 4 + 1) * 128], ktk8[pp, hd, :], vhv[pp, hd, :], True, True,
                       ["ktk8", ("vh", jb)], [("ps", ub)])
                tt(stb8[:, :, :], state[:, :, :], hsm[:, :, 2, c64].unsqueeze(2).to_broadcast([128, 8, 128]), ALU.mult,
                   STK + HSK(2), ["stb8"])
                early_M_pe(2 * c64)
                for hd in range(8):
                    ob = 4 + hd // 4
                    out = ps[ob][pp, (hd % 4) * 128:(hd % 4 + 1) * 128]
                    mm(out, qtl[:, hd, sl], stb8[:, hd, :], True, False, [("qtl", hd), "stb8"], [("ps", ob)])
                    mm(out, scs8[pp, hd, :], vhv[pp, hd, :], False, True, ["scs8", ("vh", jb)], [("ps", ob)])
                for q in range(2):
                    tt(tmp8[:, 4 * q:4 * q + 4, :], ps[6 + q][:, :].rearrange("p (h v) -> p h v", v=128),
                       hsm[:, 4 * q:4 * q + 4, 3, c64].unsqueeze(2).to_broadcast([128, 4, 128]), ALU.mult,
                       [("ps", 6 + q)] + HSK(3), [R5K[2], R5K[3]])
                tt(state[:, :, :], state[:, :, :], hsm[:, :, 4, c64].unsqueeze(2).to_broadcast([128, 8, 128]), ALU.mult,
                   STK + HSK(4), STK)
                tt(state[:, :, :], state[:, :, :], tmp8, ALU.add, STK + [R5K[2], R5K[3]], STK)
                early_M_evac(2 * c64)
                early_M_pe(2 * c64 + 1)
                for q in range(2):
                    act(otmp8[pp, 4 * q:4 * q + 4, :], ps[4 + q][pp, :].rearrange("p (h v) -> p h v", v=128), AF.Copy,
                        [("ps", 4 + q)], [R5K[0], R5K[1]])
                act(tmp8[pp, :, :], otmp8[pp, :, :], AF.Square, [R5K[0], R5K[1]], [R5K[2], R5K[3]])
                ss8, ssk, _ = smcol(8)
                t18, t1k, _ = smcol(8)
                rs8, rsk8, _ = smcol(8)
                P.op("dve", lambda e, o=ss8[pp, :], i=tmp8[pp, :, :]: e.reduce_sum(o, i, AX.X),
                     [R5K[2], R5K[3]], [ssk])
                act(t18[pp, :], ss8[pp, :], AF.Ln, [ssk, "eps"], [t1k], bias=epsc[pp, :], scale=1.0 / 128)
                act(rs8[pp, :], t18[pp, :], AF.Exp, [t1k], [rsk8], scale=-0.5)
                tt(otmp8[pp, :, :], otmp8[pp, :, :], rs8[pp, :].unsqueeze(2).to_broadcast([64, 8, 128]), ALU.mult,
                   [R5K[0], R5K[1], rsk8], [R5K[0], R5K[1]])
                tt(ogtm[pp, :, :], otmp8[pp, :, :], ghv[pp, :, :], ALU.mult, [R5K[0], R5K[1], ("ghs", jb)], ["ogtm"])
                early_M_evac(2 * c64 + 1)
                if hf == 1:
                    b3 = hrot.next()
                    pb3 = ps[b3][:, :].bitcast(BF16)
                    for hd in range(8):
                        tr(pb3[:, hd * 128:(hd + 1) * 128], ogtm[:, hd, :], ["ogtm"], [("ps", b3)])
                    act(ogT[:, :, jb * 128:(jb + 1) * 128], pb3[:, :].rearrange("p (h t) -> p h t", t=128), AF.Identity,
                        [("ps", b3), "const"], [("ogT", h) for h in range(8)], scale=hgn[:, 0:1])

            if STAGE == 15 or 140 < STAGE < 150 or STAGE > 1400:
                break
            dump('ogT', ogT[:, :, :], [('ogT', h) for h in range(8)])
            P.fence(VGK, ["mergedT"])
            mrot = Rot([(0, 1), (2, 3), (4, 5), (6, 7)])
            srot2 = Rot([0, 1, 2, 3])
            for cb in range(16):
                ba, bb_ = mrot.next()
                s3, k3 = get_unit("fm_hp", cb)
                w3 = fmv(s3, 8)
                for k in range(8):
                    mm(ps[ba][:, :], w3[:, k, :], ogT[:, k, :], k == 0, k == 7, [k3] + OGK, [("ps", ba)])
                s4, k4 = get_unit("fm_gh", cb)
                w4 = fmv(s4, 16)
                for k in range(16):
                    mm(ps[bb_][:, :], w4[:, k, :], hT[:, k, :], k == 0, k == 15, [k4, "hT"], [("ps", bb_)])
                ra = srot2.next()
                Sh = R5[:, ra, :]
                act(Sh, ps[bb_][:, :], AF.Sigmoid, [("ps", bb_)], [R5K[ra]])
                tt(Sh, Sh, ps[ba][:, :], ALU.mult, [R5K[ra], ("ps", ba)], [R5K[ra]])
                tt(mergedT[:, cb, :], S1[:, cb, :], Sh, ALU.add, [R5K[ra], "S1"], ["mergedT"])

            if STAGE == 16:
                break
            dump('mergedT', mergedT[:, :, :], ['mergedT'])
            P.fence(HKEYS, YK)

            def lhs_m(kk, j):
                return mergedT[:, kk, j * 128:(j + 1) * 128]

            ssy = ssm[:, 0:16].rearrange("p (j c) -> p j c", j=4)
            for cbo in range(4):
                def ev_o(j, bank, bk, cbo=cbo):
                    cp(ybuf[:, j, cbo * 512:(cbo + 1) * 512], bank[:, :], [bk], [("ybuf", j)])
                    act(junk[:, :], ybuf[:, j, cbo * 512:(cbo + 1) * 512], AF.Square, [("ybuf", j)], [("ssy", j), "junk"],
                        accum=ssy[:, j, cbo:cbo + 1])
                    tt(ybuf[:, j, cbo * 512:(cbo + 1) * 512], ybuf[:, j, cbo * 512:(cbo + 1) * 512],
                       G1[:, cbo * 512:(cbo + 1) * 512], ALU.mult, [("ybuf", j), ("G", 0)], [("ybuf", j)])
                tm_group("tm_wo", (cbo, 0), 512, lhs_m, ["mergedT"], 4, ev_o)
            for j in range(4):
                s1c, s1k, _ = smcol()
                P.op("dve", lambda e, o=s1c, i=ssy[:, j, :]: e.reduce_sum(o, i, AX.X), [("ssy", j)], [s1k])
                rs_, rk_ = rstd_from_ss(s1c, [s1k], 1.0 / D)
                stt(xres[:, j, :], ybuf[:, j, :], rs_, xres[:, j, :], ALU.mult, ALU.add,
                    [("ybuf", j), rk_, ("x", j)], [("x", j)])

            if STAGE == 17:
                break
            dump('x1', xres[:, :, :], [('x', j) for j in range(4)])
            P.fence(["mergedT"], XNK)
            norm_to_hT(2, 3)

            if STAGE == 18:
                break
            dump('hT2', hT[:, :, :], ['hT'])
            P.fence(XNK + ATK, ["actT"])
            fpair = Rot([(0, 1), (2, 3)])
            srot = Rot([0, 1, 2, 3])
            for half in range(2):
                hbs = list(range(0, HA)) if half == 0 else list(range(HA, NHB))
                for hb in hbs:
                    bg, bu = fpair.next()
                    sg, kg_ = get_unit("fm_fg", hb)
                    wg = fmv(sg, 16)
                    for k in range(16):
                        mm(ps[bg][:, :], wg[:, k, :], hT[:, k, :], k == 0, k == 15, [kg_, "hT"], [("ps", bg)])
                    su, ku_ = get_unit("fm_fu", hb)
                    wu = fmv(su, 16)
                    for k in range(16):
                        mm(ps[bu][:, :], wu[:, k, :], hT[:, k, :], k == 0, k == 15, [ku_, "hT"], [("ps", bu)])
                    si_ = srot.next()
                    act(R5[:, si_, :], ps[bg][:, :], AF.Silu, [("ps", bg)], [R5K[si_]])
                    tt(actT[:, hb - hbs[0], :], R5[:, si_, :], ps[bu][:, :], ALU.mult,
                       [R5K[si_], ("ps", bu)], ["actT"])
                kgs = list(range(0, HA // 4)) if half == 0 else list(range(HA // 4, NHB // 4))
                nku = len(kgs)

                def lhs_a(kk, j):
                    return actT[:, kk, j * 128:(j + 1) * 128]

                for cbo in range(4):
                    def ev_f(j, bank, bk, cbo=cbo, half=half):
                        ysl = ybuf[:, j, cbo * 512:(cbo + 1) * 512]
                        if half == 0:
                            cp(ysl, bank[:, :], [bk], [("ybuf", j)])
                        else:
                            tt(ysl, ysl, bank[:, :], ALU.add, [bk, ("ybuf", j)], [("ybuf", j)])
                            act(junk[:, :], ysl, AF.Square, [("ybuf", j)], [("ssy", j), "junk"], accum=ssy[:, j, cbo:cbo + 1])
                            tt(ysl, ysl, G2[:, cbo * 512:(cbo + 1) * 512], ALU.mult, [("ybuf", j), ("G", 1)], [("ybuf", j)])
                    banks = (4, 5, 6, 7)
                    for ki_, kgv in enumerate(kgs):
                        slot, sk = get_unit("tm_fo", cbo, kgv)
                        wv = tmv(slot, 512)
                        for j in range(4):
                            for k in range(4):
                                mm(ps[banks[j]][:, :], lhs_a(ki_ * 4 + k, j), wv[:, k, :],
                                   ki_ == 0 and k == 0, ki_ == nku - 1 and k == 3, [sk, "actT"], [("ps", banks[j])])
                    for j in range(4):
                        ev_f(j, ps[banks[j]], ("ps", banks[j]))
            for j in range(4):
                s1c, s1k, _ = smcol()
                P.op("dve", lambda e, o=s1c, i=ssy[:, j, :]: e.reduce_sum(o, i, AX.X), [("ssy", j)], [s1k])
                rs_, rk_ = rstd_from_ss(s1c, [s1k], 1.0 / D)
                stt(ybuf[:, j, :], ybuf[:, j, :], rs_, xres[:, j, :], ALU.mult, ALU.add,
                    [("ybuf", j), rk_, ("x", j)], [("ybuf", j)])
                r0 = t * T + j * 128
                P.dma("sp", lambda e, j=j, r0=r0: e.dma_start(out=y_d[r0:r0 + 128, :], in_=ybuf[:, j, :]),
                      ("yout", j), reads=[("ybuf", j)], writes=[("ydram", j)])
                if t + 1 < nt_run:
                    load_x(t + 1, j)

        P.op("sp", None, reads=[("ydram", j) for j in range(4)] + dbg_keys)
        P.emit(nc, es)
    return nc


def make_in_maps(inputs):
    x = np.asarray(inputs["x"], np.float32)
    c = np.asarray(inputs["c"], np.float32)
    pos = np.asarray(inputs["positions"], np.int32)
    w_ada = np.asarray(inputs["w_ada"], np.float32)[0]
    b_ada = np.asarray(inputs["b_ada"], np.float32)[0]
    units = build_units(np.asarray(inputs["w_in"], np.float32)[0], np.asarray(inputs["w_attn_proj"], np.float32)[0],
                        np.asarray(inputs["w_hgrn_proj"], np.float32)[0], np.asarray(inputs["w_out"], np.float32)[0],
                        np.asarray(inputs["w_ffn_in"], np.float32)[0], np.asarray(inputs["w_ffn_out"], np.float32)[0])
    wada = np.ascontiguousarray(
        w_ada.reshape(2, 8, 128, 24, 512).transpose(3, 0, 2, 1, 4).reshape(48, 128, 4096))
    bada = np.ascontiguousarray(b_ada.reshape(24, 1, 512))
    cfa = np.zeros((128, NCF), np.float32)

    def put(name, arr):
        a, b = CF[name]
        cfa[:, a:b] = arr

    put("gpre1", np.asarray(inputs["g_pre_mix"], np.float32)[0].reshape(16, 128).T)
    put("gpre2", np.asarray(inputs["g_pre_ffn"], np.float32)[0].reshape(16, 128).T)
    put("hgn", np.asarray(inputs["hg_norm"], np.float32)[0].reshape(128, 1))
    put("sinks", np.tile(np.asarray(inputs["attn_sinks"], np.float32)[0][None, :], (128, 1)))
    lb = np.asarray(inputs["hg_lower_bounds"], np.float32)
    put("lbraw", np.concatenate([lb[0].reshape(8, 128).T, lb[1].reshape(8, 128).T], axis=1))
    inv_freq = (np.float32(500000.0) ** (-np.arange(0, 16, 2, dtype=np.float32) / np.float32(16))).astype(np.float32)
    put("invf", np.tile(inv_freq[None, :], (128, 1)))
    qi = np.arange(128)[:, None]
    kj = np.arange(256)[None, :]
    rel = 128 + qi - kj
    put("maskb", np.where((rel >= 0) & (rel < 128), 0.0, -30000.0).astype(np.float32))
    rm = np.ones((128, 512), np.float32)
    rm[:, 0::64] = 0.0
    put("reset", rm)
    cbf = np.zeros((128, 256), np.float32)
    cbf[:, 0:128] = np.eye(128, dtype=np.float32)
    cbf[:, 128:256] = (np.arange(128)[None, :] >= np.arange(128)[:, None]).astype(np.float32)
    gpost = np.stack([np.tile(np.asarray(inputs["g_post_mix"], np.float32)[0][None, :], (128, 1)),
                      np.tile(np.asarray(inputs["g_post_ffn"], np.float32)[0][None, :], (128, 1))], axis=0)
    maps = []
    for b in range(8):
        maps.append({
            "x": np.ascontiguousarray(x[b]),
            "cT": np.ascontiguousarray(c[b].reshape(16, 128).T),
            "pos": np.ascontiguousarray(pos[b].reshape(16, 128).T),
            "wada": wada, "bada": bada, "wts": units, "cf": cfa, "cbf": cbf,
            "gpost": np.ascontiguousarray(gpost),
        })
    return maps


def kernel(**inputs):
    nc = build_program(_NT_RUN)
    maps = make_in_maps(inputs)
    res = run_bass_kernel_spmd(nc, maps, core_ids=list(range(8)))
    out = np.stack([np.asarray(r["y"], np.float32) for r in res.results], axis=0)
    return out
```

```python
import os
import numpy as np
from contextlib import ExitStack
import concourse.bass as bass
import concourse.mybir as mybir
from concourse.bass_utils import run_bass_kernel_spmd

F32 = mybir.dt.float32
BF16 = mybir.dt.bfloat16
F32R = mybir.dt.float32r
USE_F32R = os.environ.get('KF32R', '0') == '1'
KA = os.environ.get('KA', '1') == '1'
KB = os.environ.get('KB', '1') == '1'
KC = os.environ.get('KC', '1') == '1'
I32 = mybir.dt.int32
AF = mybir.ActivationFunctionType
ALU = mybir.AluOpType
AX = mybir.AxisListType

D = 2048
SEQ = 2048
T = 512
NT = 4
FFN = 5632
EPS = 1e-6
NSLOT = 6
HA = 24
NHB = 44
_NT_RUN = NT
_DEBUG = []


class Prog:
    ENG = ["pe", "act", "dve", "pool", "sp"]
    CH = 16000

    def __init__(self):
        self.ops = {e: [] for e in self.ENG}
        self.res = {}
        self.dma_cnt = {}

    def _deps(self, reads, writes):
        deps = set()
        for k in reads:
            st = self.res.get(k)
            if st is not None and st["w"] is not None:
                deps.add(st["w"])
        for k in writes:
            st = self.res.get(k)
            if st is not None:
                if st["w"] is not None:
                    deps.add(st["w"])
                deps.update(st["r"].values())
        return deps

    def _update(self, tok, rid, reads, writes):
        for k in reads:
            st = self.res.setdefault(k, {"w": None, "r": {}})
            st["r"][rid] = tok
        for k in writes:
            self.res[k] = {"w": tok, "r": {}}

    def op(self, eng, fn, reads=(), writes=()):
        idx = len(self.ops[eng])
        deps = self._deps(reads, writes)
        tok = ("e", eng, idx)
        self._update(tok, eng, reads, writes)
        self.ops[eng].append({"fn": fn, "deps": deps, "dma": None})
        return tok

    def dma(self, eng, fn, semkey, reads=(), writes=()):
        deps = self._deps(reads, writes)
        cnt = self.dma_cnt.get(semkey, 0) + 16
        self.dma_cnt[semkey] = cnt
        tok = ("d", semkey, cnt)
        self._update(tok, ("d", semkey), reads, writes)
        self.ops[eng].append({"fn": fn, "deps": deps, "dma": semkey})
        return tok

    def fence(self, old_keys, new_keys):
        acc = {}
        for k in old_keys:
            st = self.res.get(k)
            if st is None:
                continue
            toks = list(st["r"].items())
            if st["w"] is not None:
                w = st["w"]
                toks.append((w[1] if w[0] == "e" else ("d", w[1]), w))
            for rid, tok in toks:
                cur = acc.get(rid)
                if cur is None or tok[2] > cur[2]:
                    acc[rid] = tok
        for k in new_keys:
            st = self.res.get(k)
            merged = dict(acc)
            if st is not None:
                for rid, tok in st["r"].items():
                    cur = merged.get(rid)
                    if cur is None or tok[2] > cur[2]:
                        merged[rid] = tok
                if st["w"] is not None:
                    w = st["w"]
                    rid = w[1] if w[0] == "e" else ("d", w[1])
                    cur = merged.get(rid)
                    if cur is None or w[2] > cur[2]:
                        merged[rid] = w
            self.res[k] = {"w": None, "r": merged}

    def emit(self, nc, es):
        waited = {e: set() for e in self.ENG}
        for e in self.ENG:
            for o in self.ops[e]:
                for d in o["deps"]:
                    if d[0] == "e":
                        if d[1] == "pe" and e == "pe":
                            continue
                        waited[d[1]].add(d[2])
        semval = {}
        nsem = {}
        for e in self.ENG:
            semval[e] = {}
            for c, idx in enumerate(sorted(waited[e])):
                semval[e][idx] = (c // self.CH, c % self.CH + 1)
            nsem[e] = (len(waited[e]) + self.CH - 1) // self.CH
        esems = {e: [es.enter_context(nc.semaphore(f"s_{e}{i}")) for i in range(max(1, nsem[e]))]
                 for e in self.ENG}
        dsems = {k: es.enter_context(nc.semaphore("d_" + "_".join(str(x) for x in (k if isinstance(k, tuple) else (k,)))))
                 for k in self.dma_cnt}
        block = es.enter_context(nc.Block())
        engattr = {"pe": "tensor", "act": "scalar", "dve": "vector", "pool": "gpsimd", "sp": "sync"}

        def make(e):
            def body(eng):
                have = {}
                for idx, o in enumerate(self.ops[e]):
                    need = {}
                    for d in o["deps"]:
                        if d[0] == "e":
                            if d[1] == "pe" and e == "pe":
                                continue
                            ch, v = semval[d[1]][d[2]]
                            for c2 in range(ch):
                                sid = ("e", d[1], c2)
                                need[sid] = self.CH
                            sid = ("e", d[1], ch)
                            need[sid] = max(need.get(sid, 0), v)
                        else:
                            sid = ("d", d[1])
                            need[sid] = max(need.get(sid, 0), d[2])
                    for sid, v in need.items():
                        if v > have.get(sid, 0):
                            sem = esems[sid[1]][sid[2]] if sid[0] == "e" else dsems[sid[1]]
                            eng.wait_ge(sem, v)
                            have[sid] = v
                    if o["fn"] is None:
                        continue
                    ins = o["fn"](eng)
                    if o["dma"] is not None:
                        ins.then_inc(dsems[o["dma"]], 16)
                    elif idx in semval[e]:
                        ch, v = semval[e][idx]
                        ins.then_inc(esems[e][ch], 1)
            return body

        for e in self.ENG:
            if self.ops[e]:
                getattr(block, engattr[e])(make(e))


class Rot:
    def __init__(self, items):
        self.items = list(items)
        self.i = 0

    def next(self):
        v = self.items[self.i % len(self.items)]
        self.i += 1
        return v


W_SPLITS = dict(q_a=(0, 1024), k_a=(1024, 1152), v_a=(1152, 1280), q_h=(1280, 2304), f_h=(2304, 3328),
                i_h=(3328, 4352), g_h=(4352, 5376), gate_a=(5376, 7424), gate_h=(7424, 9472))


def unit_plan():
    plan = []
    for g in ("q0", "q1", "kv"):
        for kq in range(4):
            plan.append(("tm_" + g, kq, 0, 4 * (384 if g == "kv" else 512)))
    for g in ("i0", "i1", "g0", "g1"):
        for kq in range(4):
            plan.append(("tm_" + g, kq, 0, 2048))
    for hd in range(8):
        plan.append(("fm_qh", hd, 0, 2048))
        plan.append(("fm_fh", hd, 0, 2048))
    for cb in range(16):
        plan.append(("fm_ap", cb, 0, 1024))
        plan.append(("fm_ga", cb, 0, 2048))
    for cb in range(16):
        plan.append(("fm_hp", cb, 0, 1024))
        plan.append(("fm_gh", cb, 0, 2048))
    for cbo in range(4):
        for kq in range(4):
            plan.append(("tm_wo", cbo, kq, 2048))
    for half in range(2):
        hbs = range(0, HA) if half == 0 else range(HA, NHB)
        for hb in hbs:
            plan.append(("fm_fg", hb, 0, 2048))
            plan.append(("fm_fu", hb, 0, 2048))
        kgs = range(0, HA // 4) if half == 0 else range(HA // 4, NHB // 4)
        for cbo in range(4):
            for kg in kgs:
                plan.append(("tm_fo", cbo, kg, 2048))
    return plan


def _fm(W, cb, ncol):
    K = W.shape[0]
    nk = K // 128
    blk = W[:, cb * ncol:(cb + 1) * ncol].reshape(nk, 128, ncol).transpose(1, 0, 2)
    return blk.reshape(128, nk * ncol)


def _tm(W, k0, nk):
    ncols = W.shape[1]
    blk = W[k0 * 128:(k0 + nk) * 128, :].reshape(nk, 128, ncols).transpose(1, 0, 2)
    return blk.reshape(128, nk * ncols)


def build_units(w_in, w_attn_proj, w_hgrn_proj, w_out, w_ffn_in, w_ffn_out):
    sp = {k: w_in[:, a:b] for k, (a, b) in W_SPLITS.items()}
    ka = sp["k_a"]
    kv = np.concatenate([ka[:, 0:64], ka[:, 0:64], ka[:, 64:128], ka[:, 64:128], sp["v_a"]], axis=1)
    tmsrc = dict(q0=sp["q_a"][:, 0:512], q1=sp["q_a"][:, 512:1024], kv=kv,
                 i0=sp["i_h"][:, 0:512], i1=sp["i_h"][:, 512:1024],
                 g0=sp["g_h"][:, 0:512], g1=sp["g_h"][:, 512:1024])
    plan = unit_plan()
    units = np.zeros((len(plan), 128, 2048), np.float32)
    for u, (kind, a, b, ne) in enumerate(plan):
        if kind.startswith("tm_") and kind[3:] in tmsrc:
            arr = _tm(tmsrc[kind[3:]], a * 4, 4)
        elif kind == "fm_qh":
            arr = _fm(sp["q_h"], a, 128)
        elif kind == "fm_fh":
            arr = _fm(sp["f_h"], a, 128)
        elif kind == "fm_ap":
            arr = _fm(w_attn_proj, a, 128)
        elif kind == "fm_hp":
            arr = _fm(w_hgrn_proj, a, 128)
        elif kind == "fm_ga":
            arr = _fm(sp["gate_a"], a, 128)
        elif kind == "fm_gh":
            arr = _fm(sp["gate_h"], a, 128)
        elif kind == "tm_wo":
            arr = _tm(w_out[:, a * 512:(a + 1) * 512], b * 4, 4)
        elif kind == "fm_fg":
            arr = _fm(w_ffn_in[:, 0:FFN], a, 128)
        elif kind == "fm_fu":
            arr = _fm(w_ffn_in[:, FFN:2 * FFN], a, 128)
        elif kind == "tm_fo":
            arr = _tm(w_ffn_out[:, a * 512:(a + 1) * 512], b * 4, 4)
        else:
            raise ValueError(kind)
        assert arr.shape[1] == ne, (kind, arr.shape, ne)
        units[u, :, :ne] = arr
    return units


CF = {}
_o = 0
for _n, _w in (("gpre1", 16), ("gpre2", 16), ("hgn", 1), ("sinks", 16), ("lbraw", 16), ("invf", 8),
               ("maskb", 256), ("reset", 512)):
    CF[_n] = (_o, _o + _w)
    _o += _w
NCF = _o


def build_program(nt_run=NT, debug=()):
    nc = bass.Bass("TRN2", target_bir_lowering=False)
    plan = unit_plan()
    NU = len(plan)
    x_d = nc.dram_tensor("x", [SEQ, D], F32, kind="ExternalInput").ap()
    cT_d = nc.dram_tensor("cT", [128, 16], F32, kind="ExternalInput").ap()
    pos_d = nc.dram_tensor("pos", [128, 16], I32, kind="ExternalInput").ap()
    wada_d = nc.dram_tensor("wada", [48, 128, 4096], F32, kind="ExternalInput").ap()
    bada_d = nc.dram_tensor("bada", [24, 1, 512], F32, kind="ExternalInput").ap()
    wts_d = nc.dram_tensor("wts", [NU, 128, 2048], F32, kind="ExternalInput").ap()
    cf_d = nc.dram_tensor("cf", [128, NCF], F32, kind="ExternalInput").ap()
    cb_d = nc.dram_tensor("cbf", [128, 256], F32, kind="ExternalInput").ap()
    gpost_d = nc.dram_tensor("gpost", [2, 128, D], F32, kind="ExternalInput").ap()
    y_d = nc.dram_tensor("y", [SEQ, D], F32, kind="ExternalOutput").ap()
    dbg_d = {}

    P = Prog()
    es = ExitStack()
    with es:
        def sb(name, shape, dt):
            return es.enter_context(nc.sbuf_tensor(name, shape, dt))

        xres = sb("xres", [128, 4, D], F32)
        R3 = sb("R3", [128, 4, D], F32)
        R14 = sb("R14", [128, 16384], BF16)
        hT = sb("hT", [128, 16, T], BF16)
        R5 = sb("R5", [128, 6, T], F32)
        kTd = sb("kTd", [128, 2, SEQ], BF16)
        v_sb = sb("v_sb", [128, 16, 128], BF16)
        state = sb("state", [128, 8, 128], F32)
        G1 = sb("G1", [128, D], F32)
        G2 = sb("G2", [128, D], F32)
        ring = [sb(f"ring{i}", [128, 2048], BF16) for i in range(NSLOT)]
        cf = sb("cfs", [128, NCF], F32)
        cbf = sb("cbfs", [128, 256], BF16)
        tabs = sb("tabs", [128, 4, 16, 8], F32)
        modcol = sb("modcol", [128, 4, 16], F32)
        sm = sb("sm", [128, 512], F32)
        rope_t = sb("rope_t", [128, 4, 64], F32)
        junk = sb("junk", [128, 512], BF16)
        ones = sb("ones", [1, 128], F32)
        cT = sb("cTs", [128, 16], F32)
        posi = sb("posi", [128, 16], I32)
        lbt = sb("lbt", [128, 3, 8], F32)
        ps = [es.enter_context(nc.psum_tensor(f"ps{i}", [128, 512], F32)) for i in range(8)]

        hTf = hT[:, :, :].rearrange("p a b -> p (a b)").bitcast(F32)
        prow = hTf[0:1, 0:1024].rearrange("p (a b) -> p a b", a=2)
        brow = hTf[0:1, 1024:2048].rearrange("p (a b) -> p a b", a=2)
        R3b = R3[:, :, :].rearrange("p a b -> p (a b)").bitcast(BF16)
        R3f = R3[:, :, :].rearrange("p a b -> p (a b)")
        ident = cbf[:, 0:128]
        maskT = cbf[:, 128:256]

        def cfv(name):
            a, b = CF[name]
            return cf[:, a:b]

        dbg_keys = []

        def dump(name, ap, keys):
            if name not in debug or name in dbg_d:
                return
            dt = nc.dram_tensor("dbg_" + name, list(ap.shape), ap.dtype, kind="ExternalOutput").ap()
            dbg_d[name] = dt
            P.dma("sp", lambda e: e.dma_start(out=dt, in_=ap), ("dbg", name), reads=keys, writes=[("dbgout", name)])
            dbg_keys.append(("dbgout", name))

        def mm(out, lhsT, rhs, start, stop, reads, writes):
            P.op("pe", lambda e: e.matmul(out, lhsT, rhs, start=start, stop=stop), reads, writes)

        def tr(out, in_, reads, writes):
            P.op("pe", lambda e: e.transpose(out, in_, ident), list(reads) + ["const2"], writes)

        def act(out, in_, func, reads, writes, bias=None, scale=None, accum=None):
            kw = {}
            if bias is not None:
                kw["bias"] = bias
            if scale is not None:
                kw["scale"] = scale
            if accum is not None:
                kw["accum_out"] = accum
            P.op("act", lambda e: e.activation(out, in_, func, **kw), reads, writes)

        def tt(out, in0, in1, op, reads, writes, eng="dve"):
            P.op(eng, lambda e: e.tensor_tensor(out, in0, in1, op), reads, writes)

        def ts(out, in0, s1, s2, op0, op1, reads, writes, eng="dve"):
            if op1 is None:
                P.op(eng, lambda e: e.tensor_scalar(out, in0, s1, None, op0), reads, writes)
            else:
                P.op(eng, lambda e: e.tensor_scalar(out, in0, s1, s2, op0, op1), reads, writes)

        def stt(out, in0, scalar, in1, op0, op1, reads, writes):
            P.op("dve", lambda e: e.scalar_tensor_tensor(out, in0, scalar, in1, op0, op1), reads, writes)

        def cp(out, in_, reads, writes, eng="dve"):
            P.op(eng, lambda e: e.tensor_copy(out, in_), reads, writes)

        smc = {"i": 0}

        def smcol(n=1):
            i = smc["i"]
            if i + 8 > 480:
                i = 0
            smc["i"] = i + 8
            return sm[:, i:i + n], ("sm", i // 8), i

        ust = {"i": 0}

        def get_unit(kind, a, b=0):
            i = ust["i"]
            ust["i"] += 1
            u = i % NU
            pk, pa, pb, ne = plan[u]
            assert (pk, pa, pb) == (kind, a, b), ((pk, pa, pb), (kind, a, b))
            s = i % NSLOT
            slot = ring[s]
            P.dma("pool", lambda e: e.dma_start(out=slot[:, 0:ne], in_=wts_d[u, :, 0:ne]),
                  ("w", s), reads=(), writes=[("ring", s)])
            return slot, ("ring", s)

        def fmv(slot, nk):
            return slot[:, 0:nk * 128].rearrange("p (k c) -> p k c", k=nk)

        def tmv(slot, ncols):
            return slot[:, 0:4 * ncols].rearrange("p (k c) -> p k c", k=4)

        P.dma("sp", lambda e: e.dma_start(out=cf[:, :], in_=cf_d), "const", writes=["const"])
        P.dma("sp", lambda e: e.dma_start(out=cT[:, :], in_=cT_d), "const", writes=["const"])
        P.dma("sp", lambda e: e.dma_start(out=posi[:, :], in_=pos_d), "const", writes=["const"])
        P.dma("sp", lambda e: e.dma_start(out=R3[:, 0:2, :], in_=gpost_d.rearrange("a p d -> p a d")),
              "const", writes=["const"])
        P.dma("pool", lambda e: e.dma_start(out=cbf[:, :], in_=cb_d), "constb", writes=["constb"])
        P.res["const"]["w"] = ("d", "const", P.dma_cnt["const"])
        P.op("dve", lambda e: e.memset(ones[:, :], 1.0), writes=["ones"])
        P.op("dve", lambda e: e.memset(state[:, :, :], 0.0), writes=[("state", h) for h in range(8)])
        P.op("dve", lambda e: e.memset(sm[:, 500:501], EPS), writes=["eps"])
        epsc = sm[:, 500:501]
        P.op("dve", lambda e: e.memset(sm[:, 501:502], 0.0), reads=["constb"], writes=["const2"])

        STAGE = int(os.environ.get('KSTAGE', '99'))
        lbraw = cfv("lbraw")
        tt(lbt[:, 0, :], lbraw[:, 0:8], lbraw[:, 8:16], ALU.subtract, ["const"], ["lbt"])
        act(lbt[:, 0, :], lbt[:, 0, :], AF.Sigmoid, ["lbt"], ["lbt"])
        ts(lbt[:, 1, :], lbt[:, 0, :], -1.0, 1.0, ALU.mult, ALU.add, ["lbt"], ["lbt1"])
        ts(lbt[:, 2, :], lbt[:, 0, :], 1.0, -1.0, ALU.mult, ALU.add, ["lbt", "lbt1"], ["lbt2"])

        ang = R5[:, 0, 0:128]
        kf = R5[:, 0, 128:256]
        ki = R5[:, 0, 256:384].bitcast(I32)
        r1 = R5[:, 0, 384:512]
        r2 = R5[:, 1, 0:128]
        mk = R5[:, 1, 128:256]
        posf = R5[:, 1, 256:272]
        if STAGE < 2:
            P.emit(nc, es)
            return nc
        cp(posf, posi[:, :], ["const"], ["rp0"])
        TWO_PI = float(2.0 * np.pi)
        C1 = 6.28125
        C2 = float(2.0 * np.pi - 6.28125)
        PI = float(np.float32(np.pi))
        tt(ang.rearrange("p (n j) -> p n j", j=8), posf.unsqueeze(2).to_broadcast([128, 16, 8]),
           cfv("invf").unsqueeze(1).to_broadcast([128, 16, 8]), ALU.mult, ["rp0", "const"], ["rp1"])
        ts(ki, ang, 1.0 / TWO_PI, None, ALU.mult, None, ["rp1"], ["rp2"])
        cp(kf, ki, ["rp2"], ["rp3"])
        stt(r1, kf, -C1, ang, ALU.mult, ALU.add, ["rp3", "rp1"], ["rp4"])
        stt(r2, kf, -C2, r1, ALU.mult, ALU.add, ["rp3", "rp4"], ["rp5"])

        def wrap(dst, src, rk, wk):
            P.op("dve", lambda e: e.tensor_single_scalar(mk, src, PI, ALU.is_gt), rk, ["rpm"])
            stt(dst, mk, -TWO_PI, src, ALU.mult, ALU.add, list(rk) + ["rpm"], ["rpw"])
            P.op("dve", lambda e: e.tensor_single_scalar(mk, dst, -PI, ALU.is_lt), ["rpw"], ["rpm"])
            stt(dst, mk, TWO_PI, dst, ALU.mult, ALU.add, ["rpw", "rpm"], wk)

        rs = R5[:, 2, 0:128]
        rc = R5[:, 2, 128:256]
        wrap(rs, r2, ["rp5"], ["rp6"])
        ts(rc, rs, float(np.pi / 2), None, ALU.add, None, ["rp6"], ["rp7"])
        wrap(rc, rc, ["rp7"], ["rp8"])
        sinv = tabs[:, 3, :, :].rearrange("p n j -> p (n j)")
        cosv = tabs[:, 2, :, :].rearrange("p n j -> p (n j)")
        act(sinv, rs, AF.Sin, ["rp6"], ["tab_s"])
        act(cosv, rc, AF.Sin, ["rp8"], ["tab_c"])
        ts(tabs[:, 0, :, :].rearrange("p n j -> p (n j)"), cosv, 0.125, None, ALU.mult, None, ["tab_c"], ["tab_cq"])
        ts(tabs[:, 1, :, :].rearrange("p n j -> p (n j)"), sinv, 0.125, None, ALU.mult, None, ["tab_s"], ["tab_sq"])
        TABK = ["tab_s", "tab_c", "tab_cq", "tab_sq"]

        xres_f = xres[:, :, :].rearrange("p a b -> p (a b)")
        R14f = R14[:, :].bitcast(F32)
        NSTG = 8
        stg = [R14[:, i * 2048:(i + 1) * 2048] for i in range(NSTG)]
        stgk = [("stg", i) for i in range(NSTG)]
        cTb = sb("cTb", [128, 16], BF16)
        cp(cTb[:, :], cT[:, :], ["const"], ["cTb"])
        gp = R3
        for cbk in range(24 if STAGE >= 3 else 0):
            bank = ps[cbk % 2]
            rb = cbk % 2
            P.dma("sp", lambda e, cbk=cbk, rb=rb: e.dma_start(out=brow[:, rb, :], in_=bada_d[cbk]),
                  ("brow", rb), writes=[("brow", rb)])
            for kh in range(4):
                pi = cbk * 4 + kh
                si = pi % NSTG
                P.dma("pool", lambda e, pi=pi, si=si: e.dma_start(
                    out=stg[si], in_=wada_d[pi // 2, :, (pi % 2) * 2048:(pi % 2 + 1) * 2048]),
                      ("stg", si), writes=[stgk[si]])
                sv = stg[si].rearrange("p (k c) -> p k c", k=4)
                for k in range(4):
                    kk = kh * 4 + k
                    mm(bank[0:1, :], cTb[:, kk:kk + 1], sv[:, k, :], kk == 0, False,
                       [stgk[si], "cTb"], [("ps", cbk % 2)])
            mm(bank[0:1, :], ones[0:1, 0:1], brow[:, rb, :], False, True, [("brow", rb), "ones"], [("ps", cbk % 2)])
            cp(prow[:, rb, :], bank[0:1, :], [("ps", cbk % 2)], [("prow", rb)])
            which = cbk // 4
            q4 = cbk % 4
            bb = ps[2 + cbk % 2]
            bk = ("ps", 2 + cbk % 2)
            mm(bb[:, :], ones[0:1, :], prow[:, rb, :], True, True, [("prow", rb), "ones"], [bk])
            if which in (2, 5):
                G = G1 if which == 2 else G2
                gi = 0 if which == 2 else 1
                tt(G[:, q4 * 512:(q4 + 1) * 512], bb[:, :], gp[:, gi, q4 * 512:(q4 + 1) * 512], ALU.mult,
                   [bk, "const"], [("G", gi)])
            else:
                dgt = R5[:, 3, :].rearrange("p (c f) -> p c f", c=4)
                tt(dgt, bb[:, :].rearrange("p (c f) -> p c f", c=4), ident.unsqueeze(1).to_broadcast([128, 4, 128]),
                   ALU.mult, [bk, "const2"], ["dgt"])
                dcol, dck, _ = smcol(4)
                P.op("dve", lambda e, o=dcol, i=dgt: e.reduce_sum(o, i, AX.X), ["dgt"], [dck])
                cs = slice(q4 * 4, q4 * 4 + 4)
                if which == 0:
                    cp(modcol[:, 1, cs], dcol, [dck], ["modcol"])
                elif which == 3:
                    cp(modcol[:, 3, cs], dcol, [dck], ["modcol"])
                else:
                    gi = 0 if which == 1 else 2
                    gpre = cfv("gpre1") if which == 1 else cfv("gpre2")
                    stt(modcol[:, gi, cs], dcol, 1.0, gpre[:, cs], ALU.add, ALU.mult,
                        [dck, "const"], ["modcol"])

        XK = [("x", j) for j in range(4)]
        XNK = [("xn", j) for j in range(4)]
        ATK = [("attT", j) for j in range(4)] + [("ogT", h) for h in range(8)]
        P.fence(stgk, XK + XNK + ATK + ["mergedT", "actT"])
        P.fence([("prow", 0), ("prow", 1), ("brow", 0), ("brow", 1)], ["hT"])
        YK = [("ybuf", j) for j in range(4)]
        P.fence(["const"], YK)
        AK = []
        HK = []

        xn = R14[:, 0:8192].rearrange("p (j d) -> p j d", j=4)
        mergedT = R14[:, 0:8192].rearrange("p (c t) -> p c t", c=16)
        attT = R14[:, 8192:12288].rearrange("p (c t) -> p c t", c=8)
        ogT = R14[:, 12288:16384].rearrange("p (c t) -> p c t", c=8)
        actT = R14[:, 0:12288].rearrange("p (c t) -> p c t", c=24)
        ybuf = R3
        qtm = R3b[:, 0:4096].rearrange("p (j d) -> p j d", j=4)
        qT = R3b[:, 4096:8192].rearrange("p (c t) -> p c t", c=8)
        mlogs = [R3f[:, 4096:5120].rearrange("p (h k) -> p h k", h=4), R3f[:, 5120:6144].rearrange("p (h k) -> p h k", h=4),
                 R3f[:, 0:1024].rearrange("p (h k) -> p h k", h=4), R3f[:, 1024:2048].rearrange("p (h k) -> p h k", h=4)]
        pn = R3b[:, 12288:14336].rearrange("p (b h k) -> p b h k", b=2, h=4)
        pT = R3b[:, 14336:16384].rearrange("p (b h k) -> p b h k", b=2, h=4)
        ktm = sb("ktm", [128, 256], BF16)
        vh = R14[:, 0:4096].rearrange("p (j d) -> p j d", j=4)
        ghs = R14[:, 4096:8192].rearrange("p (j d) -> p j d", j=4)
        S1 = R3b[:, 0:8192].rearrange("p (c t) -> p c t", c=16)
        qtl = R3b[:, 8192:12288].rearrange("p (h t) -> p h t", h=8)
        ktl = R3b[:, 12288:16384].rearrange("p (h t) -> p h t", h=8)
        hsm = sb("hsm", [128, 8, 6, 8], F32)
        stb8 = sb("stb8", [128, 8, 128], BF16)
        scs8 = sb("scs8", [128, 8, 64], BF16)
        ktk8 = sb("ktk8", [128, 8, 128], BF16)
        ogtm = sb("ogtm", [128, 8, 128], BF16)
        ssm = sb("ssm", [128, 64], F32)

        AKEYS = [("qtm", j) for j in range(4)] + [("qT", j) for j in range(4)] + \
                [("mlog", b) for b in range(4)] + [("pn", b, hh) for b in range(2) for hh in range(4)] + [("pT", b) for b in range(2)]
        VGK = [("vh", j) for j in range(4)] + [("ghs", j) for j in range(4)]
        HKEYS = [("qtl", h) for h in range(8)] + [("ktl", h) for h in range(8)] + ["S1"]
        R5K = [("R5", i) for i in range(6)]
        P.fence(["rp0", "rp1", "rp2", "rp3", "rp4", "rp5", "rp6", "rp7", "rp8", "rpm", "rpw", "dgt"], R5K)

        bankset = Rot([(0, 1, 2, 3), (4, 5, 6, 7)])
        trb = Rot([4, 5])
        trb4 = Rot([4, 5, 6, 7])
        def rstd_from_ss(ss_ap, ss_keys, scale):
            t1, k1, _ = smcol()
            t2, k2, _ = smcol()
            act(t1, ss_ap, AF.Ln, list(ss_keys) + ["eps"], [k1], bias=epsc, scale=scale)
            act(t2, t1, AF.Exp, [k1], [k2], scale=-0.5)
            return t2, k2

        def load_x(tile, j):
            r0 = tile * T + j * 128
            P.dma("sp", lambda e, j=j, r0=r0: e.dma_start(out=xres[:, j, :], in_=x_d[r0:r0 + 128, :]),
                  ("xin", j), writes=[("x", j)])

        def norm_to_hT(gi, si, load_tile=None):
            for j in range(4):
                ssc, ssk, _ = smcol()
                act(xn[:, j, :], xres[:, j, :], AF.Square, [("x", j)], [("xn", j), ssk], accum=ssc)
                rs_, rk_ = rstd_from_ss(ssc, [ssk], 1.0 / D)
                ts(xn[:, j, :], xres[:, j, :], rs_, None, ALU.mult, None, [("x", j), rk_, ("xn", j)], [("xn", j)])
            P.fence(["hT"], [("hTw", 0), ("hTw", 1)])
            for c in range(16):
                b = trb4.next()
                pb = ps[b][:, :].bitcast(BF16)
                for j in range(4):
                    tr(pb[:, j * 128:(j + 1) * 128], xn[:, j, c * 128:(c + 1) * 128], [("xn", j)], [("ps", b)])
                if c % 2 == 0:
                    act(hT[:, c, :], pb[:, 0:512], AF.Identity, [("ps", b), "modcol"], [("hTw", 0)],
                        bias=modcol[:, si, c:c + 1], scale=modcol[:, gi, c:c + 1])
                else:
                    ts(hT[:, c, :], pb[:, 0:512], modcol[:, gi, c:c + 1], modcol[:, si, c:c + 1], ALU.mult, ALU.add,
                       [("ps", b), "modcol"], [("hTw", 1)])
            P.op("dve", lambda e: e.memset(sm[:, 502:503], 0.0), [("hTw", 0), ("hTw", 1)], ["hT"])

        def tm_group(kind, a, ncols, lhs_of, lhs_keys, nk_units, evac):
            banks = bankset.next()
            for kq in range(nk_units):
                if kind in ("tm_wo", "tm_fo"):
                    slot, sk = get_unit(kind, a[0], a[1] + kq)
                else:
                    slot, sk = get_unit(kind, kq)
                wv = tmv(slot, ncols)
                for j in range(4):
                    for k in range(4):
                        mm(ps[banks[j]][:, 0:ncols], lhs_of(kq * 4 + k, j), wv[:, k, :],
                           kq == 0 and k == 0, kq == nk_units - 1 and k == 3,
                           [sk] + list(lhs_keys), [("ps", banks[j])])
            for j in range(4):
                evac(j, ps[banks[j]], ("ps", banks[j]))

        def rope(psv, nh, dst, tq, n, rkeys, wkeys):
            C = tabs[:, tq, n:n + 1, :].to_broadcast([128, nh, 8])
            S_ = tabs[:, tq + 1, n:n + 1, :].to_broadcast([128, nh, 8])
            a = psv[:, :, 0:8]
            b = psv[:, :, 8:16]
            tmp = [rope_t[:, i, 0:nh * 8].rearrange("p (h e) -> p h e", e=8) for i in range(4)]
            rk = list(rkeys) + TABK
            tt(tmp[0], a, C, ALU.mult, rk, [("rt", 0)])
            tt(tmp[1], b, S_, ALU.mult, rk, [("rt", 1)])
            tt(dst[:, :, 0:8], tmp[0], tmp[1], ALU.subtract, [("rt", 0), ("rt", 1)], wkeys)
            tt(tmp[2], b, C, ALU.mult, rk, [("rt", 2)])
            tt(tmp[3], a, S_, ALU.mult, rk, [("rt", 3)])
            tt(dst[:, :, 8:16], tmp[2], tmp[3], ALU.add, [("rt", 2), ("rt", 3)], wkeys)
            act(dst[:, :, 16:64], psv[:, :, 16:64], AF.Copy, rkeys, wkeys, scale=(0.125 if tq == 0 else 1.0))

        for j in range(4):
            if nt_run > 0:
                load_x(0, j)
        for t in range(nt_run):
            P.fence(["actT"], XNK)
            norm_to_hT(0, 1, load_tile=t)

            if STAGE == 10:
                break
            dump('hT1', hT[:, :, :], ['hT'])
            dump('modcol', modcol[:, :, :], ['modcol'])
            dump('G1', G1[:, :], [('G', 0)])
            dump('tabs', tabs[:, :, :, :], TABK)
            P.fence(YK, AKEYS)
            P.fence(["actT"], ATK)

            def lhs_h(kk, j):
                return hT[:, kk, j * 128:(j + 1) * 128]

            for g in range(2):
                def ev_q(j, bank, bk, g=g):
                    n = 4 * t + j
                    rope(bank[:, :].rearrange("p (h d) -> p h d", d=64), 8,
                         qtm[:, j, g * 512:(g + 1) * 512].rearrange("p (h d) -> p h d", d=64), 0, n,
                         [bk], [("qtm", j)])
                tm_group("tm_q%d" % g, None, 512, lhs_h, ["hT"], 4, ev_q)

            def ev_kv(j, bank, bk):
                n = 4 * t + j
                rope(bank[:, 0:256].rearrange("p (h d) -> p h d", d=64), 4,
                     ktm[:, :].rearrange("p (h d) -> p h d", d=64), 2, n, [bk], ["ktm"])
                act(v_sb[:, n, :], bank[:, 256:384], AF.Copy, [bk], [("v", n)])
                b = trb.next()
                pb = ps[b][:, :].bitcast(BF16)
                for kvh in range(2):
                    tr(pb[:, kvh * 128:(kvh + 1) * 128], ktm[:, kvh * 128:(kvh + 1) * 128], ["ktm"], [("ps", b)])
                cp(kTd[:, :, n * 128:(n + 1) * 128], pb[:, 0:256].rearrange("p (a b) -> p a b", a=2),
                   [("ps", b)], [("kT", n)])
            tm_group("tm_kv", None, 384, lhs_h, ["hT"], 4, ev_kv)

            for j in range(4):
                b = trb.next()
                pb = ps[b][:, :].bitcast(BF16)
                for pr in range(8):
                    tr(pb[:, pr * 128:(pr + 1) * 128], qtm[:, j, pr * 128:(pr + 1) * 128], [("qtm", j)], [("ps", b)])
                act(qT[:, :, j * 128:(j + 1) * 128], pb[:, :].rearrange("p (c t) -> p c t", c=8), AF.Copy,
                    [("ps", b)], [("qT", j)])

            if STAGE == 11:
                break
            dump('qT', qT[:, :, :], [('qT', j) for j in range(4)])
            dump('kTd', kTd[:, :, 0:512], [('kT', n) for n in range(4)])
            dump('v_sb', v_sb[:, 0:4, :], [('v', n) for n in range(4)])
            P.fence([("qtm", j) for j in range(4)], [("mlog", 2), ("mlog", 3)])
            lgb = Rot([6, 7])
            sinkbc = cfv("sinks")
            maskb = cfv("maskb")
            groups = [(j, hg) for j in range(4) for hg in range(4)]
            gctx = {}

            def att_L(s_):
                j, hg = groups[s_]
                n = 4 * t + j
                nk = 256 if n > 0 else 128
                k0 = (n - 1) * 128 if n > 0 else 0
                buf = s_ % 4
                ml = mlogs[buf]
                c = {"nk": nk, "n": n, "j": j, "hg": hg, "buf": buf}
                c["mx"], c["mxk"], _ = smcol(4)
                c["ng"], c["ngk"], _ = smcol(4)
                c["rs"], c["rsk"], _ = smcol(4)
                c["es"], c["esk"], _ = smcol(4)
                c["rd"], c["rdk"], _ = smcol(4)
                gctx[s_] = c
                lbs = [lgb.next(), lgb.next()]
                for hh in range(4):
                    h = hg * 4 + hh
                    pr, half, kvh = h // 2, h % 2, h // 8
                    lb_ = lbs[hh % 2]
                    psl = ps[lb_][:, (hh // 2) * 256:(hh // 2) * 256 + nk]
                    mm(psl, qT[half * 64:(half + 1) * 64, pr, j * 128:(j + 1) * 128],
                       kTd[half * 64:(half + 1) * 64, kvh, k0:k0 + nk], True, True,
                       [("qT", j)] + [("kT", nn) for nn in ((n - 1, n) if n > 0 else (n,))], [("ps", lb_)])
                if KA:
                    ml2 = ml.rearrange("p (a b) k -> p a b k", b=2)
                    for b2 in range(2):
                        lb_ = lbs[b2]
                        tt(ml2[:, :, b2, 0:nk],
                           ps[lb_][:, :].rearrange("p (h k) -> p h k", h=2)[:, :, 0:nk],
                           maskb[:, 256 - nk:256].unsqueeze(1).to_broadcast([128, 2, nk]), ALU.add,
                           [("ps", lb_), "const"], [("mlog", buf)])
                else:
                    for hh in range(4):
                        lb_ = lbs[hh % 2]
                        tt(ml[:, hh, 0:nk], ps[lb_][:, (hh // 2) * 256:(hh // 2) * 256 + nk], maskb[:, 256 - nk:256],
                           ALU.add, [("ps", lb_), "const"], [("mlog", buf)])
                if KB:
                    P.op("dve", lambda e, o=c["mx"], i=ml[:, :, 0:nk]: e.reduce_max(o, i, AX.X),
                         [("mlog", buf)], [c["mxk"]])
                else:
                    for hh in range(4):
                        P.op("dve", lambda e, o=c["mx"][:, hh:hh + 1], i=ml[:, hh, 0:nk]: e.reduce_max(o, i, AX.X),
                             [("mlog", buf)], [c["mxk"]])
                tt(c["mx"], c["mx"], sinkbc[:, hg * 4:hg * 4 + 4], ALU.max, [c["mxk"], "const"], [c["mxk"]])
                ts(c["ng"], c["mx"], -1.0, None, ALU.mult, None, [c["mxk"]], [c["ngk"]])
                tt(c["es"], sinkbc[:, hg * 4:hg * 4 + 4], c["mx"], ALU.subtract, [c["mxk"], "const"], [c["esk"]])
                for hh in range(4):
                    act(ml[:, hh, 0:nk], ml[:, hh, 0:nk], AF.Exp, [("mlog", buf), c["ngk"]],
                        [("mlog", buf), c["rsk"]], bias=c["ng"][:, hh:hh + 1], accum=c["rs"][:, hh:hh + 1])
                act(c["es"], c["es"], AF.Exp, [c["esk"]], [c["esk"]])

            def att_P2(s_):
                c = gctx[s_]
                nk, buf = c["nk"], c["buf"]
                ml = mlogs[buf]
                tt(c["rd"], c["rs"], c["es"], ALU.add, [c["rsk"], c["esk"]], [c["rdk"]])
                P.op("dve", lambda e, o=c["rd"], i=c["rd"]: e.reciprocal(o, i), [c["rdk"]], [c["rdk"]])
                for hh in range(4):
                    if hh % 2 == 0:
                        act(pn[:, s_ % 2, hh, 0:nk], ml[:, hh, 0:nk], AF.Copy, [("mlog", buf), c["rdk"]],
                            [("pn", s_ % 2, hh)], scale=c["rd"][:, hh:hh + 1])
                    else:
                        ts(pn[:, s_ % 2, hh, 0:nk], ml[:, hh, 0:nk], c["rd"][:, hh:hh + 1], None, ALU.mult, None,
                           [("mlog", buf), c["rdk"]], [("pn", s_ % 2, hh)])

            def att_T(s_):
                c = gctx[s_]
                nk = c["nk"]
                b2_ = s_ % 2
                b = trb.next()
                pb = ps[b][:, :].bitcast(BF16)
                for hh in range(4):
                    for kk in range(nk // 128):
                        tr(pb[:, hh * 256 + kk * 128: hh * 256 + (kk + 1) * 128],
                           pn[:, b2_, hh, kk * 128:(kk + 1) * 128], [("pn", b2_, hh)], [("ps", b)])
                if nk == 256:
                    if KC:
                        cp(pT[:, b2_, :, :], pb[:, :].rearrange("p (h k) -> p h k", h=4), [("ps", b)], [("pT", b2_)])
                    else:
                        act(pT[:, b2_, :, :], pb[:, :].rearrange("p (h k) -> p h k", h=4), AF.Copy,
                            [("ps", b)], [("pT", b2_)])
                else:
                    act(pT[:, b2_, :, 0:128], pb[:, :].rearrange("p (h k) -> p h k", h=4)[:, :, 0:128], AF.Copy,
                        [("ps", b)], [("pT", b2_)])

            def att_PV(s_):
                c = gctx[s_]
                n, j, hg = c["n"], c["j"], c["hg"]
                b2_ = s_ % 2
                pvb = (0, 1)
                for hh in range(4):
                    h = hg * 4 + hh
                    pr, half, kvh = h // 2, h % 2, h // 8
                    ob = pvb[pr // 4]
                    out = ps[ob][half * 64:(half + 1) * 64, (pr % 4) * 128:(pr % 4 + 1) * 128]
                    if n > 0:
                        mm(out, v_sb[:, n - 1, kvh * 64:(kvh + 1) * 64], pT[:, b2_, hh, 0:128], True, False,
                           [("v", n - 1), ("pT", b2_)], [("ps", ob)])
                        mm(out, v_sb[:, n, kvh * 64:(kvh + 1) * 64], pT[:, b2_, hh, 128:256], False, True,
                           [("v", n), ("pT", b2_)], [("ps", ob)])
                    else:
                        mm(out, v_sb[:, n, kvh * 64:(kvh + 1) * 64], pT[:, b2_, hh, 0:128], True, True,
                           [("v", n), ("pT", b2_)], [("ps", ob)])
                if hg == 3:
                    for q2 in range(2):
                        act(attT[:, q2 * 4:(q2 + 1) * 4, j * 128:(j + 1) * 128],
                            ps[pvb[q2]][:, :].rearrange("p (c t) -> p c t", c=4), AF.Copy,
                            [("ps", pvb[q2])], [("attT", j)])

            P.fence(XNK, VGK)
            pieces = []

            def mk_piece(kind, units, jp, kq, dst, dkey, func):
                def piece():
                    if jp == 0:
                        units.append(get_unit(kind, kq))
                    slot, sk = units[kq]
                    wv = tmv(slot, 512)
                    for j in (2 * jp, 2 * jp + 1):
                        b = 2 + j % 2
                        for k in range(4):
                            mm(ps[b][:, :], hT[:, kq * 4 + k, j * 128:(j + 1) * 128], wv[:, k, :],
                               kq == 0 and k == 0, kq == 3 and k == 3, [sk, "hT"], [("ps", b)])
                    if kq == 3:
                        for j in (2 * jp, 2 * jp + 1):
                            b = 2 + j % 2
                            if func is None:
                                cp(dst(j), ps[b][:, :], [("ps", b)], [(dkey, j)])
                            else:
                                act(dst(j), ps[b][:, :], func, [("ps", b)], [(dkey, j)])
                return piece

            for g in range(2):
                units = []
                for jp in range(2):
                    for kq in range(4):
                        pieces.append(mk_piece("tm_i%d" % g, units, jp, kq,
                                               (lambda j, g=g: vh[:, j, g * 512:(g + 1) * 512]), "vh", None))
            for g in range(2):
                units = []
                for jp in range(2):
                    for kq in range(4):
                        pieces.append(mk_piece("tm_g%d" % g, units, jp, kq,
                                               (lambda j, g=g: ghs[:, j, g * 512:(g + 1) * 512]), "ghs", AF.Sigmoid))

            att_L(0)
            for step in range(16 + 3):
                if 0 <= step - 1 < 16:
                    att_P2(step - 1)
                if 0 <= step - 2 < 16:
                    att_T(step - 2)
                if 0 <= step - 3 < 16:
                    att_PV(step - 3)
                if step + 1 < 16:
                    att_L(step + 1)
                for _ in range(2):
                    if pieces:
                        pieces.pop(0)()
            while pieces:
                pieces.pop(0)()

            dump('attT', attT[:, :, :], [('attT', j) for j in range(4)])
            P.fence(AKEYS, HKEYS)

            if STAGE == 13:
                break
            fmb = Rot([(0, 1), (2, 3)])
            reset = cfv("reset")
            for hd in range(8):
                bq, bf = fmb.next()
                sq_, kq_ = get_unit("fm_qh", hd)
                wq = fmv(sq_, 16)
                for k in range(16):
                    mm(ps[bq][:, :], wq[:, k, :], hT[:, k, :], k == 0, k == 15, [kq_, "hT"], [("ps", bq)])
                sf_, kf_ = get_unit("fm_fh", hd)
                wf = fmv(sf_, 16)
                for k in range(16):
                    mm(ps[bf][:, :], wf[:, k, :], hT[:, k, :], k == 0, k == 15, [kf_, "hT"], [("ps", bf)])
                A = [R5[:, i, :] for i in range(6)]
                K5 = R5K
                act(A[0], ps[bf][:, :], AF.Sigmoid, [("ps", bf)], [K5[0]])
                act(A[4], ps[bq][:, :], AF.Sigmoid, [("ps", bq)], [K5[4]])
                tt(A[4], A[4], ps[bq][:, :], ALU.mult, [K5[4], ("ps", bq)], [K5[4]])
                act(A[1], A[0], AF.Ln, [K5[0], "lbt", "lbt1"], [K5[1]],
                    bias=lbt[:, 0, hd:hd + 1], scale=lbt[:, 1, hd:hd + 1])
                ts(A[2], A[0], -1.0, lbt[:, 2, hd:hd + 1], ALU.add, ALU.mult, [K5[0], "lbt2"], [K5[2]])
                P.op("dve", lambda e, o=A[3], d0=reset, d1=A[1]: e.tensor_tensor_scan(o, d0, d1, 0.0, ALU.mult, ALU.add),
                     [K5[1], "const"], [K5[3]])
                bv = A[3].rearrange("p (c t) -> p c t", t=64)
                hs = hsm[:, hd, :, :]
                hk = ("hsm", hd)
                tt(hs[:, 5, :], bv[:, :, 63], bv[:, :, 31], ALU.subtract, [K5[3]], [(hk, 5)])
                act(hs[:, 2, :], bv[:, :, 31], AF.Exp, [K5[3]], [(hk, 2)])
                act(hs[:, 3, :], hs[:, 5, :], AF.Exp, [(hk, 5)], [(hk, 3)])
                act(hs[:, 4, :], bv[:, :, 63], AF.Exp, [K5[3]], [(hk, 4)])
                tt(A[1].rearrange("p (c t) -> p c t", t=64), bv, bv[:, :, 31:32].to_broadcast([128, 8, 64]),
                   ALU.subtract, [K5[3], K5[1]], [K5[1]])
                act(A[5], A[1], AF.Exp, [K5[1]], [K5[5]])
                act(A[1], A[1], AF.Exp, [K5[1]], [K5[1]], scale=-1.0)
                tt(qtl[:, hd, :], A[4], A[5], ALU.mult, [K5[4], K5[5]], [("qtl", hd)])
                tt(ktl[:, hd, :], A[2], A[1], ALU.mult, [K5[2], K5[1]], [("ktl", hd)])

            if STAGE == 14:
                break
            dump('qtl', qtl[:, :, :], [('qtl', h) for h in range(8)])
            dump('ktl', ktl[:, :, :], [('ktl', h) for h in range(8)])
            dump('vh', vh[:, :, :], [('vh', j) for j in range(4)])
            dump('ghs', ghs[:, :, :], [('ghs', j) for j in range(4)])
            dump('hsm', hsm[:, :, :, :], [(('hsm', h), i) for h in range(8) for i in (2, 3, 4, 5)])
            hrot = Rot([0, 1])
            hgn = cfv("hgn")
            BIG = 1.0e30
            ATTK = [("attT", j) for j in range(4)]
            OGK = [("ogT", h) for h in range(8)]

            def early_M_pe(cb):
                s1, k1 = get_unit("fm_ap", cb)
                w1 = fmv(s1, 8)
                for k in range(8):
                    mm(ps[2][:, :], w1[:, k, :], attT[:, k, :], k == 0, k == 7, [k1] + ATTK, [("ps", 2)])
                s2, k2 = get_unit("fm_ga", cb)
                w2 = fmv(s2, 16)
                for k in range(16):
                    mm(ps[3][:, :], w2[:, k, :], hT[:, k, :], k == 0, k == 15, [k2, "hT"], [("ps", 3)])

            def early_M_evac(cb):
                act(R5[:, 5, :], ps[3][:, :], AF.Sigmoid, [("ps", 3)], [R5K[5]])
                tt(S1[:, cb, :], R5[:, 5, :], ps[2][:, :], ALU.mult, [R5K[5], ("ps", 2)], ["S1"])

            STK = [("state", h) for h in range(8)]
            HSK = lambda i: [(("hsm", h), i) for h in range(8)]
            QK = [("qtl", h) for h in range(8)]
            KK = [("ktl", h) for h in range(8)]
            otmp8 = R5[:, 0:2, :].rearrange("p a (h v) -> p (a h) v", v=128)
            tmp8 = R5[:, 2:4, :].rearrange("p a (h v) -> p (a h) v", v=128)
            for c64 in range(8):
                jb, hf = c64 // 2, c64 % 2
                sl = slice(c64 * 64, (c64 + 1) * 64)
                pp = slice(hf * 64, (hf + 1) * 64)
                vhv = vh[:, jb, :].rearrange("p (h v) -> p h v", v=128)
                ghv = ghs[:, jb, :].rearrange("p (h v) -> p h v", v=128)
                bA = hrot.next()
                for hd in range(8):
                    mm(ps[bA][pp, hd * 64:(hd + 1) * 64], ktl[:, hd, sl], qtl[:, hd, sl], True, True,
                       [("ktl", hd), ("qtl", hd)], [("ps", bA)])
                ctmp = R5[pp, 4, :]
                ts(ctmp, ps[bA][pp, :], BIG, -BIG, ALU.min, ALU.max, [("ps", bA)], [R5K[4]])
                tt(scs8[pp, :, :], ctmp.rearrange("p (h t) -> p h t", t=64),
                   maskT[pp, hf * 64:(hf + 1) * 64].unsqueeze(1).to_broadcast([64, 8, 64]), ALU.mult,
                   [R5K[4], "const2"], ["scs8"])
                bB = hrot.next()
                pbB = ps[bB][:, :].bitcast(BF16)
                for hd in range(8):
                    tr(pbB[pp, hd * 128:(hd + 1) * 128], ktl[:, hd, sl], [("ktl", hd)], [("ps", bB)])
                act(ktk8[pp, :, :], pbB[pp, :].rearrange("p (h k) -> p h k", k=128), AF.Copy, [("ps", bB)], ["ktk8"])
                for hd in range(8):
                    ub = 6 + hd // 4
                    mm(ps[ub][:, (hd % 4) * 128:(hd % 4 + 1) * 128], ktk8[pp, hd, :], vhv[pp, hd, :], True, True,
                       ["ktk8", ("vh", jb)], [("ps", ub)])
                tt(stb8[:, :, :], state[:, :, :], hsm[:, :, 2, c64].unsqueeze(2).to_broadcast([128, 8, 128]), ALU.mult,
                   STK + HSK(2), ["stb8"])
                early_M_pe(2 * c64)
                for hd in range(8):
                    ob = 4 + hd // 4
                    out = ps[ob][pp, (hd % 4) * 128:(hd % 4 + 1) * 128]
                    mm(out, qtl[:, hd, sl], stb8[:, hd, :], True, False, [("qtl", hd), "stb8"], [("ps", ob)])
                    mm(out, scs8[pp, hd, :], vhv[pp, hd, :], False, True, ["scs8", ("vh", jb)], [("ps", ob)])
                for q in range(2):
                    tt(tmp8[:, 4 * q:4 * q + 4, :], ps[6 + q][:, :].rearrange("p (h v) -> p h v", v=128),
                       hsm[:, 4 * q:4 * q + 4, 3, c64].unsqueeze(2).to_broadcast([128, 4, 128]), ALU.mult,
                       [("ps", 6 + q)] + HSK(3), [R5K[2], R5K[3]])
                tt(state[:, :, :], state[:, :, :], hsm[:, :, 4, c64].unsqueeze(2).to_broadcast([128, 8, 128]), ALU.mult,
                   STK + HSK(4), STK)
                tt(state[:, :, :], state[:, :, :], tmp8, ALU.add, STK + [R5K[2], R5K[3]], STK)
                early_M_evac(2 * c64)
                early_M_pe(2 * c64 + 1)
                for q in range(2):
                    act(otmp8[pp, 4 * q:4 * q + 4, :], ps[4 + q][pp, :].rearrange("p (h v) -> p h v", v=128), AF.Copy,
                        [("ps", 4 + q)], [R5K[0], R5K[1]])
                act(tmp8[pp, :, :], otmp8[pp, :, :], AF.Square, [R5K[0], R5K[1]], [R5K[2], R5K[3]])
                ss8, ssk, _ = smcol(8)
                t18, t1k, _ = smcol(8)
                rs8, rsk8, _ = smcol(8)
                P.op("dve", lambda e, o=ss8[pp, :], i=tmp8[pp, :, :]: e.reduce_sum(o, i, AX.X),
                     [R5K[2], R5K[3]], [ssk])
                act(t18[pp, :], ss8[pp, :], AF.Ln, [ssk, "eps"], [t1k], bias=epsc[pp, :], scale=1.0 / 128)
                act(rs8[pp, :], t18[pp, :], AF.Exp, [t1k], [rsk8], scale=-0.5)
                tt(otmp8[pp, :, :], otmp8[pp, :, :], rs8[pp, :].unsqueeze(2).to_broadcast([64, 8, 128]), ALU.mult,
                   [R5K[0], R5K[1], rsk8], [R5K[0], R5K[1]])
                tt(ogtm[pp, :, :], otmp8[pp, :, :], ghv[pp, :, :], ALU.mult, [R5K[0], R5K[1], ("ghs", jb)], ["ogtm"])
                early_M_evac(2 * c64 + 1)
                if hf == 1:
                    b3 = hrot.next()
                    pb3 = ps[b3][:, :].bitcast(BF16)
                    for hd in range(8):
                        tr(pb3[:, hd * 128:(hd + 1) * 128], ogtm[:, hd, :], ["ogtm"], [("ps", b3)])
                    act(ogT[:, :, jb * 128:(jb + 1) * 128], pb3[:, :].rearrange("p (h t) -> p h t", t=128), AF.Identity,
                        [("ps", b3), "const"], [("ogT", h) for h in range(8)], scale=hgn[:, 0:1])

            if STAGE == 15 or 140 < STAGE < 150 or STAGE > 1400:
                break
            dump('ogT', ogT[:, :, :], [('ogT', h) for h in range(8)])
            P.fence(VGK, ["mergedT"])
            mrot = Rot([(0, 1), (2, 3), (4, 5), (6, 7)])
            srot2 = Rot([0, 1, 2, 3])
            for cb in range(16):
                ba, bb_ = mrot.next()
                s3, k3 = get_unit("fm_hp", cb)
                w3 = fmv(s3, 8)
                for k in range(8):
                    mm(ps[ba][:, :], w3[:, k, :], ogT[:, k, :], k == 0, k == 7, [k3] + OGK, [("ps", ba)])
                s4, k4 = get_unit("fm_gh", cb)
                w4 = fmv(s4, 16)
                for k in range(16):
                    mm(ps[bb_][:, :], w4[:, k, :], hT[:, k, :], k == 0, k == 15, [k4, "hT"], [("ps", bb_)])
                ra = srot2.next()
                Sh = R5[:, ra, :]
                act(Sh, ps[bb_][:, :], AF.Sigmoid, [("ps", bb_)], [R5K[ra]])
                tt(Sh, Sh, ps[ba][:, :], ALU.mult, [R5K[ra], ("ps", ba)], [R5K[ra]])
                tt(mergedT[:, cb, :], S1[:, cb, :], Sh, ALU.add, [R5K[ra], "S1"], ["mergedT"])

            if STAGE == 16:
                break
            dump('mergedT', mergedT[:, :, :], ['mergedT'])
            P.fence(HKEYS, YK)

            def lhs_m(kk, j):
                return mergedT[:, kk, j * 128:(j + 1) * 128]

            ssy = ssm[:, 0:16].rearrange("p (j c) -> p j c", j=4)
            for cbo in range(4):
                def ev_o(j, bank, bk, cbo=cbo):
                    cp(ybuf[:, j, cbo * 512:(cbo + 1) * 512], bank[:, :], [bk], [("ybuf", j)])
                    act(junk[:, :], ybuf[:, j, cbo * 512:(cbo + 1) * 512], AF.Square, [("ybuf", j)], [("ssy", j), "junk"],
                        accum=ssy[:, j, cbo:cbo + 1])
                    tt(ybuf[:, j, cbo * 512:(cbo + 1) * 512], ybuf[:, j, cbo * 512:(cbo + 1) * 512],
                       G1[:, cbo * 512:(cbo + 1) * 512], ALU.mult, [("ybuf", j), ("G", 0)], [("ybuf", j)])
                tm_group("tm_wo", (cbo, 0), 512, lhs_m, ["mergedT"], 4, ev_o)
            for j in range(4):
                s1c, s1k, _ = smcol()
                P.op("dve", lambda e, o=s1c, i=ssy[:, j, :]: e.reduce_sum(o, i, AX.X), [("ssy", j)], [s1k])
                rs_, rk_ = rstd_from_ss(s1c, [s1k], 1.0 / D)
                stt(xres[:, j, :], ybuf[:, j, :], rs_, xres[:, j, :], ALU.mult, ALU.add,
                    [("ybuf", j), rk_, ("x", j)], [("x", j)])

            if STAGE == 17:
                break
            dump('x1', xres[:, :, :], [('x', j) for j in range(4)])
            P.fence(["mergedT"], XNK)
            norm_to_hT(2, 3)

            if STAGE == 18:
                break
            dump('hT2', hT[:, :, :], ['hT'])
            P.fence(XNK + ATK, ["actT"])
            fpair = Rot([(0, 1), (2, 3)])
            srot = Rot([0, 1, 2, 3])
            for half in range(2):
                hbs = list(range(0, HA)) if half == 0 else list(range(HA, NHB))
                for hb in hbs:
                    bg, bu = fpair.next()
                    sg, kg_ = get_unit("fm_fg", hb)
                    wg = fmv(sg, 16)
                    for k in range(16):
                        mm(ps[bg][:, :], wg[:, k, :], hT[:, k, :], k == 0, k == 15, [kg_, "hT"], [("ps", bg)])
                    su, ku_ = get_unit("fm_fu", hb)
                    wu = fmv(su, 16)
                    for k in range(16):
                        mm(ps[bu][:, :], wu[:, k, :], hT[:, k, :], k == 0, k == 15, [ku_, "hT"], [("ps", bu)])
                    si_ = srot.next()
                    act(R5[:, si_, :], ps[bg][:, :], AF.Silu, [("ps", bg)], [R5K[si_]])
                    tt(actT[:, hb - hbs[0], :], R5[:, si_, :], ps[bu][:, :], ALU.mult,
                       [R5K[si_], ("ps", bu)], ["actT"])
                kgs = list(range(0, HA // 4)) if half == 0 else list(range(HA // 4, NHB // 4))
                nku = len(kgs)

                def lhs_a(kk, j):
                    return actT[:, kk, j * 128:(j + 1) * 128]

                for cbo in range(4):
                    def ev_f(j, bank, bk, cbo=cbo, half=half):
                        ysl = ybuf[:, j, cbo * 512:(cbo + 1) * 512]
                        if half == 0:
                            cp(ysl, bank[:, :], [bk], [("ybuf", j)])
                        else:
                            tt(ysl, ysl, bank[:, :], ALU.add, [bk, ("ybuf", j)], [("ybuf", j)])
                            act(junk[:, :], ysl, AF.Square, [("ybuf", j)], [("ssy", j), "junk"], accum=ssy[:, j, cbo:cbo + 1])
                            tt(ysl, ysl, G2[:, cbo * 512:(cbo + 1) * 512], ALU.mult, [("ybuf", j), ("G", 1)], [("ybuf", j)])
                    banks = (4, 5, 6, 7) if cbo % 2 == 0 else (0, 1, 2, 3)
                    for ki_, kgv in enumerate(kgs):
                        slot, sk = get_unit("tm_fo", cbo, kgv)
                        wv = tmv(slot, 512)
                        for j in range(4):
                            for k in range(4):
                                mm(ps[banks[j]][:, :], lhs_a(ki_ * 4 + k, j), wv[:, k, :],
                                   ki_ == 0 and k == 0, ki_ == nku - 1 and k == 3, [sk, "actT"], [("ps", banks[j])])
                    for j in range(4):
                        ev_f(j, ps[banks[j]], ("ps", banks[j]))
            for j in range(4):
                s1c, s1k, _ = smcol()
                P.op("dve", lambda e, o=s1c, i=ssy[:, j, :]: e.reduce_sum(o, i, AX.X), [("ssy", j)], [s1k])
                rs_, rk_ = rstd_from_ss(s1c, [s1k], 1.0 / D)
                stt(ybuf[:, j, :], ybuf[:, j, :], rs_, xres[:, j, :], ALU.mult, ALU.add,
                    [("ybuf", j), rk_, ("x", j)], [("ybuf", j)])
                r0 = t * T + j * 128
                P.dma("sp", lambda e, j=j, r0=r0: e.dma_start(out=y_d[r0:r0 + 128, :], in_=ybuf[:, j, :]),
                      ("yout", j), reads=[("ybuf", j)], writes=[("ydram", j)])
                if t + 1 < nt_run:
                    load_x(t + 1, j)

        P.op("sp", None, reads=[("ydram", j) for j in range(4)] + dbg_keys)
        P.emit(nc, es)
    return nc


def make_in_maps(inputs):
    x = np.asarray(inputs["x"], np.float32)
    c = np.asarray(inputs["c"], np.float32)
    pos = np.asarray(inputs["positions"], np.int32)
    w_ada = np.asarray(inputs["w_ada"], np.float32)[0]
    b_ada = np.asarray(inputs["b_ada"], np.float32)[0]
    units = build_units(np.asarray(inputs["w_in"], np.float32)[0], np.asarray(inputs["w_attn_proj"], np.float32)[0],
                        np.asarray(inputs["w_hgrn_proj"], np.float32)[0], np.asarray(inputs["w_out"], np.float32)[0],
                        np.asarray(inputs["w_ffn_in"], np.float32)[0], np.asarray(inputs["w_ffn_out"], np.float32)[0])
    wada = np.ascontiguousarray(
        w_ada.reshape(2, 8, 128, 24, 512).transpose(3, 0, 2, 1, 4).reshape(48, 128, 4096))
    bada = np.ascontiguousarray(b_ada.reshape(24, 1, 512))
    cfa = np.zeros((128, NCF), np.float32)

    def put(name, arr):
        a, b = CF[name]
        cfa[:, a:b] = arr

    put("gpre1", np.asarray(inputs["g_pre_mix"], np.float32)[0].reshape(16, 128).T)
    put("gpre2", np.asarray(inputs["g_pre_ffn"], np.float32)[0].reshape(16, 128).T)
    put("hgn", np.asarray(inputs["hg_norm"], np.float32)[0].reshape(128, 1))
    put("sinks", np.tile(np.asarray(inputs["attn_sinks"], np.float32)[0][None, :], (128, 1)))
    lb = np.asarray(inputs["hg_lower_bounds"], np.float32)
    put("lbraw", np.concatenate([lb[0].reshape(8, 128).T, lb[1].reshape(8, 128).T], axis=1))
    inv_freq = (np.float32(500000.0) ** (-np.arange(0, 16, 2, dtype=np.float32) / np.float32(16))).astype(np.float32)
    put("invf", np.tile(inv_freq[None, :], (128, 1)))
    qi = np.arange(128)[:, None]
    kj = np.arange(256)[None, :]
    rel = 128 + qi - kj
    put("maskb", np.where((rel >= 0) & (rel < 128), 0.0, -30000.0).astype(np.float32))
    rm = np.ones((128, 512), np.float32)
    rm[:, 0::64] = 0.0
    put("reset", rm)
    cbf = np.zeros((128, 256), np.float32)
    cbf[:, 0:128] = np.eye(128, dtype=np.float32)
    cbf[:, 128:256] = (np.arange(128)[None, :] >= np.arange(128)[:, None]).astype(np.float32)
    gpost = np.stack([np.tile(np.asarray(inputs["g_post_mix"], np.float32)[0][None, :], (128, 1)),
                      np.tile(np.asarray(inputs["g_post_ffn"], np.float32)[0][None, :], (128, 1))], axis=0)
    maps = []
    for b in range(8):
        maps.append({
            "x": np.ascontiguousarray(x[b]),
            "cT": np.ascontiguousarray(c[b].reshape(16, 128).T),
            "pos": np.ascontiguousarray(pos[b].reshape(16, 128).T),
            "wada": wada, "bada": bada, "wts": units, "cf": cfa, "cbf": cbf,
            "gpost": np.ascontiguousarray(gpost),
        })
    return maps


def kernel(**inputs):
    nc = build_program(_NT_RUN)
    maps = make_in_maps(inputs)
    res = run_bass_kernel_spmd(nc, maps, core_ids=list(range(8)))
    out = np.stack([np.asarray(r["y"], np.float32) for r in res.results], axis=0)
    return out
```
